# Optimizing a Trainium2 kernel written in Bass

```python
import jax, jax.numpy as jnp
from jax import lax
import numpy as np

D_MODEL = 2048
BATCH = 2
SEQ = 8192
DEPTH = 4

CTX_LEN = 256
GRID_W = 64
CHUNK = 128
EPS = 1e-6

A_HEADS = 4
A_DIM = 128
A_WIDTH = A_HEADS * A_DIM
B_HEADS = 8
B_KV_HEADS = 2
HEAD_DIM = 128
B_GROUP = B_HEADS // B_KV_HEADS
B_WIDTH = B_HEADS * HEAD_DIM
B_KV_WIDTH = B_KV_HEADS * HEAD_DIM
WINDOW = 128
ROPE_THETA = 10000.0
ROPE_FREQS = HEAD_DIM // 4
C_GROUPS = 4
C_DIM = 128
C_WIDTH = C_GROUPS * C_DIM

MIX_WIDTH = A_WIDTH + B_WIDTH + C_WIDTH
SPLIT_SIZES = (A_WIDTH, A_WIDTH, A_WIDTH, B_WIDTH, B_KV_WIDTH, B_KV_WIDTH, B_WIDTH, C_WIDTH, C_WIDTH)
IN_COLS = 3 * A_WIDTH + 2 * B_WIDTH + 2 * B_KV_WIDTH + 2 * C_WIDTH
NEG_INF = -1e30

kernel_name = "hybrid_gmlp_swa_fnet_diffusion_trunk"


def rmsnorm(x, g):
    xf = x.astype(jnp.float32)
    y = xf * lax.rsqrt(jnp.mean(xf * xf, axis=-1, keepdims=True) + EPS)
    return (y * g.astype(jnp.float32)).astype(x.dtype)


def split_cols(proj):
    idx, acc = [], 0
    for s in SPLIT_SIZES[:-1]:
        acc += s
        idx.append(acc)
    return jnp.split(proj, idx, axis=-1)


def rope_tables(row, col):
    freqs = ROPE_THETA ** (-jnp.arange(ROPE_FREQS, dtype=jnp.float32) / ROPE_FREQS)
    ang = jnp.stack([row.astype(jnp.float32)[:, None] * freqs,
                     col.astype(jnp.float32)[:, None] * freqs], axis=1)
    ang = jnp.broadcast_to(ang[:, :, None, :], (ang.shape[0], 2, 2, ROPE_FREQS))
    ang = ang.reshape(ang.shape[0], HEAD_DIM)
    return jnp.cos(ang), jnp.sin(ang)


def rope_2d(x, cos, sin):
    shp = x.shape
    xf = x.astype(jnp.float32).reshape(*shp[:-1], 2, 2, ROPE_FREQS)
    rot = jnp.stack([-xf[..., 1, :], xf[..., 0, :]], axis=-2).reshape(shp)
    out = xf.reshape(shp) * cos[None, :, None, :] + rot * sin[None, :, None, :]
    return out.astype(x.dtype)


def chunk_gmlp(u, v, g_sgu, w_s, b_s):
    bsz, t, _ = v.shape
    nc = t // CHUNK
    vn = rmsnorm(v, g_sgu).reshape(bsz, nc, CHUNK, A_HEADS, A_DIM)
    mixed = jnp.einsum('hpq,bcqhd->bcphd', w_s, vn) + b_s.T[None, None, :, :, None]
    return u * mixed.reshape(bsz, t, A_WIDTH)


def fourier_mix(xc, w_f, b_f):
    bsz, t, _ = xc.shape
    xg = xc.reshape(bsz, t, C_GROUPS, C_DIM).astype(jnp.float32)
    y = jnp.fft.fft2(xg, axes=(1, 3), norm='ortho').real.astype(xc.dtype)
    y = jnp.einsum('btgc,gcd->btgd', y, w_f) + b_f
    return y.reshape(bsz, t, C_WIDTH)


def sink_column(sink, shape_prefix):
    s = sink.astype(jnp.float32).reshape(B_KV_HEADS, B_GROUP)
    return jnp.broadcast_to(s[:, :, None, None], shape_prefix + (1,))


def window_attention(q, k, v, k_ctx, v_ctx, sink):
    bsz, s, _, dh = q.shape
    nb = s // CHUNK
    scale = dh ** -0.5
    pad = ((0, 0), (CHUNK, CHUNK), (0, 0), (0, 0))
    kp = jnp.pad(k, pad).reshape(bsz, nb + 2, CHUNK, B_KV_HEADS, dh)
    vp = jnp.pad(v, pad).reshape(bsz, nb + 2, CHUNK, B_KV_HEADS, dh)
    kw = jnp.concatenate([kp[:, :-2], kp[:, 1:-1], kp[:, 2:]], axis=2)
    vw = jnp.concatenate([vp[:, :-2], vp[:, 1:-1], vp[:, 2:]], axis=2)
    qb = q.reshape(bsz, nb, CHUNK, B_KV_HEADS, B_GROUP, dh)
    s_loc = jnp.einsum('bnqkgd,bnjkd->bnkgqj', qb, kw).astype(jnp.float32) * scale
    s_ctx = jnp.einsum('bnqkgd,bmkd->bnkgqm', qb, k_ctx).astype(jnp.float32) * scale
    a = jnp.arange(CHUNK)[:, None]
    j = jnp.arange(3 * CHUNK)[None, :]
    blk = jnp.arange(nb)[:, None, None]
    key_pos = blk * CHUNK - CHUNK + j
    mask = (jnp.abs(j - CHUNK - a) <= WINDOW)[None] & (key_pos >= 0) & (key_pos < s)
    s_loc = jnp.where(mask[None, :, None, None], s_loc, NEG_INF)
    sk = sink_column(sink, (bsz, nb, B_KV_HEADS, B_GROUP, CHUNK))
    n_ctx = k_ctx.shape[1]
    p = jax.nn.softmax(jnp.concatenate([sk, s_ctx, s_loc], axis=-1), axis=-1)
    p_ctx = p[..., 1:1 + n_ctx].astype(v.dtype)
    p_loc = p[..., 1 + n_ctx:].astype(v.dtype)
    o = (jnp.einsum('bnkgqm,bmkd->bnqkgd', p_ctx, v_ctx)
         + jnp.einsum('bnkgqj,bnjkd->bnqkgd', p_loc, vw))
    return o.reshape(bsz, s, B_WIDTH)


def context_attention(q, k, v, sink):
    bsz, n, _, dh = q.shape
    qg = q.reshape(bsz, n, B_KV_HEADS, B_GROUP, dh)
    s = jnp.einsum('blkgd,bmkd->bkglm', qg, k).astype(jnp.float32) * (dh ** -0.5)
    sk = sink_column(sink, (bsz, B_KV_HEADS, B_GROUP, n))
    p = jax.nn.softmax(jnp.concatenate([sk, s], axis=-1), axis=-1)[..., 1:].astype(v.dtype)
    o = jnp.einsum('bkglm,bmkd->blkgd', p, v)
    return o.reshape(bsz, n, B_WIDTH)


def mixer_branches(proj, attn_fn, g_sgu, w_s, b_s, w_f, b_f):
    a_u, a_v, a_g, b_q, b_k, b_v, b_g, c_x, c_g = split_cols(proj)
    y_a = chunk_gmlp(jax.nn.gelu(a_u), jax.nn.gelu(a_v), g_sgu, w_s, b_s) * jax.nn.silu(a_g)
    y_b = attn_fn(b_q, b_k, b_v) * jax.nn.silu(b_g)
    y_c = fourier_mix(c_x, w_f, b_f) * jax.nn.silu(c_g)
    return jnp.concatenate([y_a, y_b, y_c], axis=-1)


def heads(t, n_heads):
    return t.reshape(t.shape[0], t.shape[1], n_heads, HEAD_DIM)


def setup_inputs(seed: int = 0) -> dict:
    key = jax.random.key(seed)
    ks = jax.random.split(key, 20)
    nrm = jax.random.normal
    f32 = jnp.float32
    return {
        "x": nrm(ks[0], (BATCH, SEQ, D_MODEL), f32),
        "c": nrm(ks[1], (BATCH, D_MODEL), f32),
        "ctx": nrm(ks[2], (BATCH, CTX_LEN, D_MODEL), f32),
        "c_ctx": nrm(ks[3], (D_MODEL,), f32),
        "w_mod": nrm(ks[4], (DEPTH, D_MODEL, 3 * D_MODEL), f32) * D_MODEL ** -0.5,
        "b_mod": nrm(ks[5], (DEPTH, 3 * D_MODEL), f32) * 0.01,
        "g_pre": 1.0 + 0.01 * nrm(ks[6], (DEPTH, D_MODEL), f32),
        "g_post": 1.0 + 0.01 * nrm(ks[7], (DEPTH, D_MODEL), f32),
        "w_in": nrm(ks[8], (DEPTH, D_MODEL, IN_COLS), f32) * D_MODEL ** -0.5,
        "w_out": nrm(ks[9], (DEPTH, MIX_WIDTH, D_MODEL), f32) * MIX_WIDTH ** -0.5,
        "g_sgu": 1.0 + 0.01 * nrm(ks[10], (DEPTH, A_WIDTH), f32),
        "w_sgu": nrm(ks[11], (DEPTH, A_HEADS, CHUNK, CHUNK), f32) * CHUNK ** -0.5,
        "b_sgu": nrm(ks[12], (DEPTH, A_HEADS, CHUNK), f32) * 0.01,
        "sink": nrm(ks[13], (DEPTH, B_HEADS), f32) * 0.5,
        "w_fourier": nrm(ks[14], (DEPTH, C_GROUPS, C_DIM, C_DIM), f32) * C_DIM ** -0.5,
        "b_fourier": nrm(ks[15], (DEPTH, C_GROUPS, C_DIM), f32) * 0.01,
    }


def reference(x, c, ctx, c_ctx, w_mod, b_mod, g_pre, g_post, w_in, w_out,
              g_sgu, w_sgu, b_sgu, sink, w_fourier, b_fourier):
    s = x.shape[1]
    rows = s // GRID_W
    row = jnp.repeat(jnp.arange(rows), GRID_W)
    col = jnp.tile(jnp.arange(GRID_W), rows)
    cos, sin = rope_tables(row, col)
    silu_c = jax.nn.silu(c)
    silu_cc = jax.nn.silu(c_ctx)

    for l in range(DEPTH):
        last = l == DEPTH - 1
        shift, scale, gate = jnp.split(silu_c @ w_mod[l] + b_mod[l], 3, axis=-1)
        shift_c, scale_c, gate_c = jnp.split(silu_cc @ w_mod[l] + b_mod[l], 3, axis=-1)

        h = rmsnorm(x, g_pre[l]) * (1.0 + scale[:, None]) + shift[:, None]
        h_c = rmsnorm(ctx, g_pre[l]) * (1.0 + scale_c) + shift_c
        proj = h @ w_in[l]
        proj_c = h_c @ w_in[l]

        pc = split_cols(proj_c)
        k_ctx = heads(pc[4], B_KV_HEADS)
        v_ctx = heads(pc[5], B_KV_HEADS)
        sink_l = sink[l]

        def latent_attn(bq, bk, bv):
            q = rope_2d(heads(bq, B_HEADS), cos, sin)
            k = rope_2d(heads(bk, B_KV_HEADS), cos, sin)
            return window_attention(q, k, heads(bv, B_KV_HEADS), k_ctx, v_ctx, sink_l)

        y = mixer_branches(proj, latent_attn, g_sgu[l], w_sgu[l], b_sgu[l],
                           w_fourier[l], b_fourier[l])
        y = rmsnorm(y @ w_out[l], g_post[l])
        x_new = x + gate[:, None] * y

        if not last:
            def ctx_attn(bq, bk, bv):
                return context_attention(heads(bq, B_HEADS), heads(bk, B_KV_HEADS),
                                         heads(bv, B_KV_HEADS), sink_l)

            y_c = mixer_branches(proj_c, ctx_attn, g_sgu[l], w_sgu[l], b_sgu[l],
                                 w_fourier[l], b_fourier[l])
            y_c = rmsnorm(y_c @ w_out[l], g_post[l])
            ctx = ctx + gate_c * y_c
        x = x_new
    return x
```

```python
import math
import numpy as np
import ml_dtypes
import concourse.bass as bass
import concourse.mybir as mybir
from concourse.bass_utils import run_bass_kernel_spmd

F32 = mybir.dt.float32
BF16 = mybir.dt.bfloat16
AF = mybir.ActivationFunctionType
ALU = mybir.AluOpType
AX = mybir.AxisListType
NPBF = ml_dtypes.bfloat16

D = 2048
NCH = 18
NTOK = NCH * 128
INC = 5120
EPS = 1e-6
SCALE = 128 ** -0.5
NEG = -30000.0


class Buf:
    __slots__ = ("name", "w", "r")

    def __init__(self, name=""):
        self.name = name
        self.w = None
        self.r = {}


class Sched:
    ENGS = ("pe", "act", "dve", "pool", "sp")
    NDMA = 8

    def __init__(self, nc):
        self.nc = nc
        self.q = {e: [] for e in self.ENGS}
        self.semh = {}
        self.cnt = {}
        for e in self.ENGS:
            self.semh[e] = nc.alloc_semaphore(name=f"sem_{e}")
            self.cnt[e] = 0
        self.seen = {e: {} for e in self.ENGS}
        self.dq = {}
        for e in ("sp", "pool", "act"):
            keys = []
            for i in range(self.NDMA):
                k = f"d_{e}_{i}"
                self.semh[k] = nc.alloc_semaphore(name=k)
                self.cnt[k] = 0
                keys.append(k)
            self.dq[e] = [keys, 0]

    def _wait(self, eng, need):
        for s, v in need.items():
            if v <= 0:
                continue
            if s == "pe" and eng == "pe":
                continue
            if self.seen[eng].get(s, 0) >= v:
                continue
            self.seen[eng][s] = v
            h = self.semh[s]
            self.q[eng].append(lambda E, h=h, v=v: E.wait_ge(h, v))

    def _deps(self, reads, writes):
        need = {}

        def add(ev):
            if ev is None:
                return
            s, v = ev
            if need.get(s, 0) < v:
                need[s] = v
        for b in reads:
            add(b.w)
        for b in writes:
            add(b.w)
            for s, v in b.r.items():
                add((s, v))
        return need

    def _mark(self, ev, reads, writes):
        s, v = ev
        for b in reads:
            if b.r.get(s, 0) < v:
                b.r[s] = v
        for b in writes:
            b.w = ev
            b.r = {}

    def op(self, eng, fn, reads=(), writes=(), signal=True):
        self._wait(eng, self._deps(reads, writes))
        if signal:
            self.cnt[eng] += 1
            ev = (eng, self.cnt[eng])
            h = self.semh[eng]
            self.q[eng].append(lambda E, fn=fn, h=h: fn(E).then_inc(h, 1))
        else:
            ev = (eng, self.cnt[eng] + 1)
            self.q[eng].append(lambda E, fn=fn: fn(E))
        self._mark(ev, reads, writes)
        return ev

    def dma(self, eng, out, in_, reads=(), writes=(), **kw):
        keys, nxt = self.dq[eng]
        k = keys[nxt % self.NDMA]
        self.dq[eng][1] = nxt + 1
        need = self._deps(reads, writes)
        if self.cnt[k] > 0:
            need[k] = max(need.get(k, 0), self.cnt[k])
        self._wait(eng, need)
        self.cnt[k] += 16
        ev = (k, self.cnt[k])
        h = self.semh[k]
        self.q[eng].append(
            lambda E, out=out, in_=in_, h=h, kw=kw: E.dma_start(out=out, in_=in_, **kw).then_inc(h, 16))
        self._mark(ev, reads, writes)
        return ev

    def coll(self, kind, groups, in_ap, out_ap, reads=(), writes=()):
        k = "cc"
        if k not in self.semh:
            self.semh[k] = self.nc.alloc_semaphore(name="sem_cc")
            self.cnt[k] = 0
        need = self._deps(reads, writes)
        if self.cnt[k] > 0:
            need[k] = max(need.get(k, 0), self.cnt[k])
        self._wait("pool", need)
        self.cnt[k] += 1
        ev = (k, self.cnt[k])
        h = self.semh[k]
        self.q["pool"].append(lambda E: E.collective_compute(
            kind, ALU.bypass, replica_groups=groups, ins=[in_ap.opt()], outs=[out_ap.opt()]).then_inc(h, 1))
        self._mark(ev, reads, writes)
        return ev

    def dma_dyn(self, eng, fn, reads=(), writes=()):
        keys, nxt = self.dq[eng]
        k = keys[nxt % self.NDMA]
        self.dq[eng][1] = nxt + 1
        need = self._deps(reads, writes)
        if self.cnt[k] > 0:
            need[k] = max(need.get(k, 0), self.cnt[k])
        self._wait(eng, need)
        self.cnt[k] += 16
        ev = (k, self.cnt[k])
        h = self.semh[k]

        def emit(E, fn=fn, h=h):
            o, i = fn(self)
            E.dma_start(out=o, in_=i).then_inc(h, 16)
        self.q[eng].append(emit)
        self._mark(ev, reads, writes)
        return ev

    def barrier(self, skip_cc=False):
        need = {k: v for k, v in self.cnt.items() if v > 0 and not (skip_cc and k == "cc")}
        for e in self.ENGS:
            self._wait(e, dict(need))

    def wait_all(self, eng, bufs):
        need = {}
        for b in bufs:
            if b.w is not None:
                s, v = b.w
                need[s] = max(need.get(s, 0), v)
        self._wait(eng, need)

    def emit(self):
        nc = self.nc
        q = self.q
        with nc.Block() as block:
            @block.tensor
            def _(E):
                for f in q["pe"]:
                    f(E)

            @block.scalar
            def _(E):
                for f in q["act"]:
                    f(E)

            @block.vector
            def _(E):
                for f in q["dve"]:
                    f(E)

            @block.gpsimd
            def _(E):
                for f in q["pool"]:
                    f(E)

            @block.sync
            def _(E):
                pid = E.partition_id()
                j4 = E.snap(pid % 4, min_val=0, max_val=3)
                self.JG = E.snap(j4 * 8192, min_val=0, max_val=3 * 8192)
                self.PREV = E.snap(((j4 + 3) % 4) * 256, min_val=0, max_val=768)
                self.NEXT = E.snap(((j4 + 1) % 4) * 256, min_val=0, max_val=768)
                for f in q["sp"]:
                    f(E)


class Ctx:
    def __init__(self, nc):
        self.nc = nc
        self.S = Sched(nc)
        self.n = 0
        self.scopes = [[]]

    def push(self):
        self.scopes.append([])

    def pop(self, keep_first=0):
        sc = self.scopes.pop()
        for g in reversed(sc[keep_first:]):
            g.__exit__(None, None, None)
        return sc[:keep_first]

    def sb(self, shape, dt, name=None):
        self.n += 1
        g = self.nc.sbuf_tensor(f"{name or 't'}_{self.n}", list(shape), dt)
        t = g.__enter__()
        self.scopes[-1].append(g)
        return t, Buf(name)

    def ps(self, shape, dt=F32, name=None):
        self.n += 1
        g = self.nc.psum_tensor(f"{name or 'p'}_{self.n}", list(shape), dt)
        t = g.__enter__()
        self.scopes[-1].append(g)
        return t, Buf(name)

    def din(self, name, shape, dt):
        return self.nc.dram_tensor(name, list(shape), dt, kind="ExternalInput").ap()

    def dout(self, name, shape, dt):
        return self.nc.dram_tensor(name, list(shape), dt, kind="ExternalOutput").ap()

    def dscr(self, name, shape, dt):
        return self.nc.dram_tensor(name, list(shape), dt).ap(), Buf(name)

    def phase_end(self, skip_cc=False, keep_first=0):
        self.S.barrier(skip_cc)
        kept = self.pop(keep_first)
        self.push()
        self.scopes[-1].extend(kept)


def new_nc():
    return bass.Bass("TRN2", target_bir_lowering=False)


def ph_M(C, t):
    S = C.S
    cT, bm, mo = t.cT, t.bm, t.mo
    ct, b_ct = C.sb([128, 16, 2], F32, "ct")
    st, b_st = C.sb([128, 16, 2], F32, "st")
    bt, b_bt = C.sb([2, 6144], F32, "bt")
    ot, b_ot = C.sb([2, 6144], F32, "ot")
    wt = [C.sb([128, 16, 512], F32, f"wt{i}") for i in range(2)]
    pp = [C.ps([128, 512], F32, f"pp{i}") for i in range(2)]
    b_mo = Buf("mo")
    S.dma("sp", ct[:], cT, writes=[b_ct])
    S.dma("sp", bt[:], bm, writes=[b_bt])
    S.op("act", lambda E: E.activation(out=st[:], in_=ct[:], func=AF.Silu), reads=[b_ct], writes=[b_st])
    for nb in range(12):
        w, b_w = wt[nb % 2]
        p, b_p = pp[nb % 2]
        wv = t.wm[nb // 3].rearrange("(k p) n -> p k n", p=128)
        vb = nb % 3
        for hh in range(2):
            S.dma("sp" if hh == 0 else "act", w[:, 8 * hh:8 * hh + 8, :],
                  wv[:, 8 * hh:8 * hh + 8, vb * 512:(vb + 1) * 512], writes=[b_w])
        for k in range(16):
            S.op("pe", lambda E, k=k, w=w, p=p: E.matmul(out=p[0:2, :], lhsT=st[:, k, :], rhs=w[:, k, :],
                                                         start=(k == 0), stop=(k == 15)),
                 reads=[b_st, b_w], writes=[b_p], signal=(k == 15))
        S.op("dve", lambda E, p=p, nb=nb: E.tensor_tensor(out=ot[:, nb * 512:(nb + 1) * 512], in0=p[0:2, :],
                                                            in1=bt[:, nb * 512:(nb + 1) * 512], op=ALU.add),
             reads=[b_p, b_bt], writes=[b_ot])
    S.dma("sp", mo, ot[:], reads=[b_ot], writes=[b_mo])


TM_BLOCKS = [
    ("a_u", 0), ("a_v", 512), ("a_g", 1024), ("q0", 1536), ("q1", 2048), ("kv", 2560),
    ("g0", 3072), ("g1", 3584), ("c_g", 4608)]


def ph_A1(C, t, l, xin):
    S = C.S
    gpre, win, gsgu, wf = t.gpre[l:l + 1], t.win[l], t.gsgu[l:l + 1], t.wf[l]
    ident, cs128, cosT, sinS = t.ident, t.cs128, t.cosT, t.sinS
    P = t.P
    b_P, b_UV = Buf("P"), Buf("UV")

    idn, b_idn = C.sb([128, 128], BF16, "idn")
    cs, b_cs = C.sb([128, 256], BF16, "cs")
    wfb, b_wfb = C.sb([128, 4, 128], BF16, "wfb")
    wcs, b_wcs = C.sb([128, 4, 256], BF16, "wcs")
    gsB = [C.sb([128, D], F32, "gsB")]
    shB = [C.sb([128, D], F32, "shB")]
    gsg, b_gsg = C.sb([128, 512], F32, "gsg")
    cosA, b_cos = C.sb([128, 16, 128], F32, "cosA")
    sinA, b_sin = C.sb([128, 16, 128], F32, "sinA")
    hT, _ = C.sb([128, 16, NTOK], BF16, "hT")
    b_hT = [Buf(f"hT{c}") for c in range(NCH)]
    cxT, b_cxT = C.sb([128, 4, NTOK], BF16, "cxT")

    S.dma("sp", idn[:], ident, writes=[b_idn])
    S.dma("sp", cs[:], cs128, writes=[b_cs])
    S.dma("pool", wfb[:], wf, writes=[b_wfb])
    S.dma("sp", gsg[:], gsgu[0, :].partition_broadcast(128), writes=[b_gsg])
    S.dma("sp", cosA[:], cosT.rearrange("(c p) d -> p c d", p=128), writes=[b_cos])
    S.dma("sp", sinA[:], sinS.rearrange("(c p) d -> p c d", p=128), writes=[b_sin])

    xt = [C.sb([128, D], F32, f"xt{i}") for i in range(3)]
    tmp, b_tmp = xt[2]
    junk, b_junk = C.sb([128, 1024], BF16, "junk")
    hb = [C.sb([128, D], BF16, f"hb{i}") for i in range(2)]
    b_hB = [Buf("hB0"), Buf("hB1")]
    ssq = [C.sb([128, 4], F32, f"ssq{i}") for i in range(2)]
    _pb = [C.ps([128, 512], F32, f"ptrb{i}") for i in range(2)]
    ptr = [(_pb[i][0][:].bitcast(BF16)[:, 0:512].rearrange("p (j d) -> p j d", d=128), _pb[i][1]) for i in range(2)]
    pp = [C.ps([128, 512], F32, f"pp{i}") for i in range(3)]
    puv = [C.ps([128, 2, 256], F32, f"puv{i}") for i in range(2)]

    def load_mod(i):
        g_t, b_g = gsB[0]
        s_t, b_s = shB[0]
        S.dma("sp", tmp[:], gpre[0, :].partition_broadcast(128), writes=[b_tmp])
        S.dma("sp", g_t[:].rearrange("p (j c) -> p j c", c=512), t.moG5[:, i, l, 1, :].partition_broadcast(128),
              writes=[b_g])
        S.dma("sp", s_t[:].rearrange("p (j c) -> p j c", c=512), t.moG5[:, i, l, 0, :].partition_broadcast(128),
              writes=[b_s])
        S.op("dve", lambda E, g_t=g_t: E.scalar_tensor_tensor(out=g_t[:], in0=g_t[:], scalar=1.0, in1=tmp[:],
                                                               op0=ALU.add, op1=ALU.mult),
             reads=[b_g, b_tmp], writes=[b_g])

    for g in range(4):
        p, b_p = pp[g % 3]
        for j in range(2):
            S.op("pe", lambda E, g=g, j=j, p=p: E.matmul(out=p[:, j * 128:(j + 1) * 128],
                                                         lhsT=cs[:, j * 128:(j + 1) * 128], rhs=wfb[:, g, :],
                                                         start=True, stop=True),
                 reads=[b_cs, b_wfb], writes=[b_p], signal=(j == 1))
        S.op("dve", lambda E, g=g, p=p: E.tensor_copy(out=wcs[:, g, :], in_=p[:, 0:256]),
             reads=[b_p], writes=[b_wcs])

    wb = [C.sb([128, 16, 512], BF16, f"wb{i}") for i in range(2)]
    stg = [C.sb([128, 512], BF16, f"stg{i}") for i in range(2)]
    r1, b_r1 = C.sb([128, 512], F32, "r1")
    r2, b_r2 = C.sb([128, 512], F32, "r2")
    gv, b_gv = C.sb([128, 512], F32, "gv")
    sq2, b_sq2 = r1, b_r1
    sv, b_sv = C.sb([128, 4], F32, "sv")
    winv = win.rearrange("(k p) n -> p k n", p=128)
    blocks = TM_BLOCKS + [("c_x", 4096)]

    def load_w(bi):
        w, b_w = wb[bi % 2]
        c0 = blocks[bi][1]
        for q4 in range(4):
            S.dma("pool", w[:, 4 * q4:4 * q4 + 4, :], winv[:, 4 * q4:4 * q4 + 4, c0:c0 + 512], writes=[b_w])

    load_w(0)

    order = [16, 17] + list(range(16))

    def normA(n):
        c = order[n]
        if c == 16:
            load_mod(1)
        if c == 0:
            load_mod(0)
        x_t, b_x = xt[n % 3]
        h_t, b_h = hb[n % 2]
        b_h2 = b_hB[n % 2]
        s_t, b_s = ssq[n % 2]
        S.dma("sp", x_t[:], xin[c * 128:(c + 1) * 128, :], writes=[b_x])
        S.op("act", lambda E: E.activation(out=junk[:], in_=x_t[:, 0:1024], func=AF.Square, accum_out=s_t[:, 3:4]),
             reads=[b_x], writes=[b_junk, b_s])
        S.op("act", lambda E: E.activation(out=junk[:], in_=x_t[:, 1024:D], func=AF.Square, accum_out=s_t[:, 1:2]),
             reads=[b_x], writes=[b_junk, b_s])
        S.op("dve", lambda E: E.tensor_tensor(out=s_t[:, 0:1], in0=s_t[:, 3:4], in1=s_t[:, 1:2], op=ALU.add),
             reads=[b_s], writes=[b_s])
        S.op("act", lambda E: E.activation(out=s_t[:, 1:2], in_=s_t[:, 0:1], func=AF.Sqrt, bias=EPS, scale=1.0 / D),
             reads=[b_s], writes=[b_s])
        S.op("dve", lambda E: E.reciprocal(out=s_t[:, 2:3], in_=s_t[:, 1:2]), reads=[b_s], writes=[b_s])
        S.op("dve", lambda E: E.scalar_tensor_tensor(out=x_t[:], in0=x_t[:], scalar=s_t[:, 2:3], in1=gsB[0][0][:],
                                                     op0=ALU.mult, op1=ALU.mult),
             reads=[b_x, b_s, gsB[0][1]], writes=[b_x])
        S.op("pool", lambda E: E.tensor_tensor(out=h_t[:, 0:1024], in0=x_t[:, 0:1024], in1=shB[0][0][:, 0:1024], op=ALU.add),
             reads=[b_x, shB[0][1]], writes=[b_h])
        S.op("dve", lambda E: E.tensor_tensor(out=h_t[:, 1024:D], in0=x_t[:, 1024:D], in1=shB[0][0][:, 1024:D], op=ALU.add),
             reads=[b_x, shB[0][1]], writes=[b_h2])

    def normB(n):
        c = order[n]
        h_t, b_h = hb[n % 2]
        b_h2 = b_hB[n % 2]
        for g4 in range(4):
            p, b_p = ptr[g4 % 2]
            for j in range(4):
                k = 4 * g4 + j
                S.op("pe", lambda E, k=k, j=j, p=p: E.transpose(out=p[:, j, :], in_=h_t[:, k * 128:(k + 1) * 128],
                                                               identity=idn[:]),
                     reads=[b_h if k < 8 else b_h2, b_idn], writes=[b_p], signal=(j == 3))
            if g4 % 2 == 0:
                S.op("dve", lambda E, g4=g4, p=p: E.tensor_copy(out=hT[:, 4 * g4:4 * g4 + 4, c * 128:(c + 1) * 128], in_=p[:]),
                     reads=[b_p], writes=[b_hT[c]])
            else:
                S.op("act", lambda E, g4=g4, p=p: E.activation(out=hT[:, 4 * g4:4 * g4 + 4, c * 128:(c + 1) * 128],
                                                               in_=p[:], func=AF.Copy),
                     reads=[b_p], writes=[b_hT[c]])

    normA(0)
    for n in range(NCH):
        if n + 1 < NCH:
            normA(n + 1)
        normB(n)

    def rope(src, dst, c, ncols, bs, bd):
        nh = ncols // 128
        cb = cosA[:, c, :].unsqueeze(1).to_broadcast([128, nh, 128])
        S.op("dve", lambda E: E.tensor_tensor(out=r1[:, 0:ncols].rearrange("p (h d) -> p h d", d=128),
                                              in0=src.rearrange("p (h d) -> p h d", d=128), in1=cb, op=ALU.mult),
             reads=[bs, b_cos], writes=[b_r1])
        s5 = src.rearrange("p (h a b f) -> p h a b f", a=2, b=2, f=32)
        r5 = r2[:, 0:ncols].rearrange("p (h a b f) -> p h a b f", a=2, b=2, f=32)
        n4 = sinA[:, c, :].rearrange("p (a b f) -> p a b f", a=2, b=2, f=32)
        for bdst in range(2):
            sb_ = n4[:, :, bdst, :].unsqueeze(1).to_broadcast([128, nh, 2, 32])
            S.op("dve", lambda E, bdst=bdst, sb_=sb_: E.tensor_tensor(
                out=r5[:, :, :, bdst, :], in0=s5[:, :, :, 1 - bdst, :], in1=sb_, op=ALU.mult),
                reads=[bs, b_sin], writes=[b_r2])
        S.op("dve", lambda E: E.tensor_tensor(out=dst, in0=r1[:, 0:ncols], in1=r2[:, 0:ncols], op=ALU.add),
             reads=[b_r1, b_r2], writes=[bd])

    for bi, (bname, c0) in enumerate(blocks):
        if bi + 1 < len(blocks):
            load_w(bi + 1)
        w, b_w = wb[bi % 2]
        if bname != "c_x":
            for c in range(NCH):
                p, b_p = pp[c % 3]
                st_t, b_st = stg[c % 2]
                for k in range(16):
                    S.op("pe", lambda E, k=k, c=c, w=w, p=p: E.matmul(out=p[:], lhsT=hT[:, k, c * 128:(c + 1) * 128],
                                                                    rhs=w[:, k, :], start=(k == 0), stop=(k == 15)),
                         reads=[b_hT[c], b_w], writes=[b_p], signal=(k == 15))
                lat = c < 16
                if bname == "a_u":
                    S.op("act", lambda E, p=p, st_t=st_t: E.activation(out=st_t[:], in_=p[:], func=AF.Gelu_apprx_tanh),
                         reads=[b_p], writes=[b_st])
                elif bname == "a_v":
                    S.op("act", lambda E, p=p: E.activation(out=gv[:], in_=p[:], func=AF.Gelu_apprx_tanh),
                         reads=[b_p], writes=[b_gv])
                    S.op("act", lambda E: E.activation(out=sq2[:], in_=gv[:], func=AF.Square, accum_out=sv[:, 0:1]),
                         reads=[b_gv], writes=[b_sq2, b_sv])
                    S.op("act", lambda E: E.activation(out=sv[:, 1:2], in_=sv[:, 0:1], func=AF.Sqrt, bias=EPS,
                                                       scale=1.0 / 512),
                         reads=[b_sv], writes=[b_sv])
                    S.op("dve", lambda E: E.reciprocal(out=sv[:, 2:3], in_=sv[:, 1:2]), reads=[b_sv], writes=[b_sv])
                    S.op("dve", lambda E, st_t=st_t: E.scalar_tensor_tensor(out=st_t[:], in0=gv[:], scalar=sv[:, 2:3],
                                                                           in1=gsg[:], op0=ALU.mult, op1=ALU.mult),
                         reads=[b_gv, b_sv, b_gsg], writes=[b_st])
                elif bname in ("a_g", "g0", "g1", "c_g"):
                    S.op("act", lambda E, p=p, st_t=st_t: E.activation(out=st_t[:], in_=p[:], func=AF.Silu),
                         reads=[b_p], writes=[b_st])
                elif bname in ("q0", "q1"):
                    if lat:
                        rope(p[:], st_t[:], c, 512, b_p, b_st)
                    else:
                        S.op("act", lambda E, p=p, st_t=st_t: E.activation(out=st_t[:], in_=p[:], func=AF.Copy),
                             reads=[b_p], writes=[b_st])
                elif bname == "kv":
                    if lat:
                        rope(p[:, 0:256], st_t[:, 0:256], c, 256, b_p, b_st)
                        S.op("act", lambda E, p=p, st_t=st_t: E.activation(out=st_t[:, 256:512], in_=p[:, 256:512],
                                                                           func=AF.Copy),
                             reads=[b_p], writes=[b_st])
                    else:
                        S.op("act", lambda E, p=p, st_t=st_t: E.activation(out=st_t[:], in_=p[:], func=AF.Copy),
                             reads=[b_p], writes=[b_st])
                S.dma("sp", P[c * 128:(c + 1) * 128, c0:c0 + 512], st_t[:], reads=[b_st], writes=[b_P])
                if bname == "kv" and c in (0, 15):
                    e0 = 0 if c == 0 else 128
                    S.dma("sp", t.kvE[e0:e0 + 128, :], st_t[:], reads=[b_st], writes=[b_UV])
        else:
            tblocks = [(0, 512), (512, 512), (1024, 512), (1536, 512), (2048, 256)]
            it = 0
            for g in range(4):
                for (t0, tn) in tblocks:
                    p, b_p = pp[it % 3]
                    it += 1
                    cl = sorted(set(range(t0 // 128, (t0 + tn) // 128)))
                    for k in range(16):
                        S.op("pe", lambda E, k=k, g=g, t0=t0, tn=tn, w=w, p=p: E.matmul(
                            out=p[:, 0:tn], lhsT=w[:, k, g * 128:(g + 1) * 128], rhs=hT[:, k, t0:t0 + tn],
                            start=(k == 0), stop=(k == 15)),
                            reads=[b_hT[c] for c in cl] + [b_w], writes=[b_p], signal=(k == 15))
                    if it % 2 == 0:
                        S.op("dve", lambda E, g=g, t0=t0, tn=tn, p=p: E.tensor_copy(out=cxT[:, g, t0:t0 + tn],
                                                                                  in_=p[:, 0:tn]),
                             reads=[b_p], writes=[b_cxT])
                    else:
                        S.op("act", lambda E, g=g, t0=t0, tn=tn, p=p: E.activation(out=cxT[:, g, t0:t0 + tn],
                                                                                 in_=p[:, 0:tn], func=AF.Copy),
                             reads=[b_p], writes=[b_cxT])
            uvs = [C.sb([128, 4, 256], BF16, f"uvs{i}") for i in range(2)]
            for c in range(NCH):
                u_t, b_u = uvs[c % 2]
                for half in range(2):
                    p, b_p = puv[half]
                    for gg in range(2):
                        g = 2 * half + gg
                        S.op("pe", lambda E, g=g, gg=gg, c=c, p=p: E.matmul(
                            out=p[:, gg, :], lhsT=cxT[:, g, c * 128:(c + 1) * 128], rhs=wcs[:, g, :],
                            start=True, stop=True),
                            reads=[b_cxT, b_wcs], writes=[b_p], signal=(gg == 1))
                    if half == 0:
                        S.op("dve", lambda E, p=p, u_t=u_t: E.tensor_copy(out=u_t[:, 0:2, :], in_=p[:]),
                             reads=[b_p], writes=[b_u])
                    else:
                        S.op("act", lambda E, p=p, u_t=u_t: E.activation(out=u_t[:, 2:4, :], in_=p[:], func=AF.Copy),
                             reads=[b_p], writes=[b_u])
                if c < 16:
                    S.dma("sp", t.UVl3[:, c * 128:(c + 1) * 128, :].rearrange("g p c -> p g c"), u_t[:],
                          reads=[b_u], writes=[b_UV])
                else:
                    S.dma("sp", t.UVc[(c - 16) * 128:(c - 15) * 128, :, :], u_t[:], reads=[b_u], writes=[b_UV])


def _bf(a):
    return np.ascontiguousarray(np.asarray(a, dtype=np.float32)).astype(NPBF)


def const_ident():
    return _bf(np.eye(128))


def const_cs128():
    n = np.arange(128)
    ang = 2 * np.pi * np.outer(n, n) / 128.0
    return _bf(np.concatenate([np.cos(ang), np.sin(ang)], axis=1) / math.sqrt(128.0))


def rope_tables(j):
    t = np.arange(2048 * j, 2048 * j + 2048)
    row = (t // 64).astype(np.float32)
    col = (t % 64).astype(np.float32)
    freqs = (np.float32(10000.0) ** (-np.arange(32, dtype=np.float32) / np.float32(32))).astype(np.float32)
    ang = np.stack([row[:, None] * freqs, col[:, None] * freqs], axis=1)
    ang = np.broadcast_to(ang[:, :, None, :], (2048, 2, 2, 32)).astype(np.float32)
    cos = np.cos(ang).astype(np.float32)
    sin = np.sin(ang).astype(np.float32).copy()
    sin[:, :, 0, :] *= -1.0
    return np.ascontiguousarray(cos.reshape(2048, 128)), np.ascontiguousarray(sin.reshape(2048, 128))


def const_masks(first, last):
    i = np.arange(128)[:, None]
    jj = np.arange(128)[None, :]
    mid = np.zeros((128, 384), np.float32)
    mid[:, 0:128] = np.where(jj >= i, 0.0, NEG)
    mid[:, 256:384] = np.where(jj <= i, 0.0, NEG)
    mf = mid.copy()
    ml = mid.copy()
    if first:
        mf[:, 0:128] = NEG
    if last:
        ml[:, 256:384] = NEG
    return np.ascontiguousarray(np.stack([mf, mid, ml]))


def ph_A2(C, t, l, chunks):
    S = C.S
    P, wsT, bsT, sink, masks, ident, yab = t.P, t.wsT[l], t.bsT[l], t.sink[l:l + 1], t.masks, t.ident, t.yab
    b_y = Buf("yab")

    idn, b_idn = C.sb([128, 128], BF16, "idn")
    wsb, b_wsb = C.sb([128, 4, 128], BF16, "wsb")
    bst, b_bst = C.sb([128, 4], F32, "bst")
    skB, b_sk = C.sb([128, 8], F32, "skB")
    mk, b_mk = C.sb([128, 3, 384], F32, "mk")
    kv, b_kv = C.sb([128, 21, 512], BF16, "kv")
    kT, b_kT = C.sb([128, 2, 21 * 128], BF16, "kT")
    S.dma("sp", idn[:], ident, writes=[b_idn])
    wsf, b_wsf = C.sb([128, 4, 128], F32, "wsf")
    S.dma("sp", wsf[:], wsT, writes=[b_wsf])
    S.op("act", lambda E: E.activation(out=wsb[:], in_=wsf[:], func=AF.Copy), reads=[b_wsf], writes=[b_wsb])
    S.dma("sp", bst[:], bsT, writes=[b_bst])
    S.dma("sp", skB[:], sink[0, :].partition_broadcast(128), writes=[b_sk])
    S.dma("sp", mk[:], masks.rearrange("m p k -> p m k"), writes=[b_mk])
    Pv = P.rearrange("(c p) n -> p c n", p=128)
    S.dma("sp", kv[:, 0:2, :], Pv[:, 16:18, 2560:3072], writes=[b_kv])
    b_kvH, b_kTH = Buf("kvH"), Buf("kTH")
    S.dma_dyn("sp", lambda E: (kv[:, 2, :], t.kvG[bass.ds(E.PREV, 256), :][128:256, :]),
              reads=[t.b_kvG], writes=[b_kvH])
    S.dma("sp", kv[:, 3:19, :], Pv[:, 0:16, 2560:3072], writes=[b_kv])
    S.dma_dyn("sp", lambda E: (kv[:, 19, :], t.kvG[bass.ds(E.NEXT, 256), :][0:128, :]),
              reads=[t.b_kvG], writes=[b_kvH])

    bk = [C.ps([128, 512], F32, f"bk{i}") for i in range(8)]

    def bfv(i):
        return bk[i][0][:].bitcast(BF16)
    psC = [(bk[0][0][:, 0:256], bk[0][1]), (bk[2][0][:, 0:256], bk[2][1])]
    psL = [(bk[1][0], bk[1][1]), (bk[3][0], bk[3][1])]
    pPT = [(bfv(4)[:, 0:640].rearrange("p (j d) -> p j d", d=128), bk[4][1]),
           (bfv(5)[:, 0:640].rearrange("p (j d) -> p j d", d=128), bk[5][1])]
    pO = [(bk[6][0][:, 0:128], bk[6][1]), (bk[6][0][:, 128:256], bk[6][1])]
    pa, b_pa = bk[7][0], bk[7][1]
    ptr = [(bfv(7)[:, 0:512].rearrange("p (j d) -> p j d", d=128), bk[7][1]),
           (bfv(7)[:, 512:1024].rearrange("p (j d) -> p j d", d=128), bk[7][1])]

    it = 0
    own = [0, 1] + list(range(3, 19))
    for g in range(2):
        for s0 in range(0, len(own), 4):
            sl = own[s0:s0 + 4]
            ns = len(sl)
            p, b_p = ptr[it % 2]
            it += 1
            for jj, sidx in enumerate(sl):
                S.op("pe", lambda E, g=g, sidx=sidx, jj=jj, p=p: E.transpose(out=p[:, jj, :],
                                                                           in_=kv[:, sidx, g * 128:(g + 1) * 128],
                                                                           identity=idn[:]),
                     reads=[b_kv, b_idn], writes=[b_p], signal=(jj == ns - 1))
            for jj, sidx in enumerate(sl):
                pass
            runs = []
            for jj, sidx in enumerate(sl):
                if runs and runs[-1][1] + runs[-1][2] == sidx:
                    runs[-1][2] += 1
                else:
                    runs.append([jj, sidx, 1])
            for (j0, sstart, n_) in runs:
                S.op("dve", lambda E, g=g, j0=j0, sstart=sstart, n_=n_, p=p: E.tensor_copy(
                    out=kT[:, g, sstart * 128:(sstart + n_) * 128].rearrange("p (s d) -> p s d", d=128),
                    in_=p[:, j0:j0 + n_, :]), reads=[b_p], writes=[b_kT])
    p, b_p = ptr[it % 2]
    for jj, (g, sidx) in enumerate([(0, 2), (1, 2), (0, 19), (1, 19)]):
        S.op("pe", lambda E, g=g, sidx=sidx, jj=jj, p=p: E.transpose(out=p[:, jj, :], in_=kv[:, sidx, g * 128:(g + 1) * 128],
                                                                   identity=idn[:]),
             reads=[b_kvH, b_idn], writes=[b_p], signal=(jj == 3))
    for jj, (g, sidx) in enumerate([(0, 2), (1, 2), (0, 19), (1, 19)]):
        S.op("act", lambda E, g=g, sidx=sidx, jj=jj, p=p: E.activation(out=kT[:, g, sidx * 128:(sidx + 1) * 128],
                                                                     in_=p[:, jj, :], func=AF.Copy),
             reads=[b_p], writes=[b_kTH])

    NCB = 3
    pin = [C.sb([128, 1536], BF16, f"pin{i}") for i in range(NCB)]
    qg = [C.sb([128, 2, 1024], BF16, f"qg{i}") for i in range(NCB)]
    qT = [C.sb([128, 8, 128], BF16, f"qT{i}") for i in range(NCB)]
    yst = [C.sb([128, 1536], BF16, f"yst{i}") for i in range(NCB)]
    ta, b_ta = C.sb([128, 512], F32, "ta")
    NS_, NP_, NT_, NM_ = 4, 4, 3, 12
    Ss = [C.sb([128, 640], F32, f"Ss{i}") for i in range(NS_)]
    Pb = [C.sb([128, 640], BF16, f"Pb{i}") for i in range(NP_)]
    PT = [C.sb([128, 5, 128], BF16, f"PT{i}") for i in range(NT_)]
    sm = [C.sb([128, 8], F32, f"sm{i}") for i in range(NM_)]

    items = [(c, h) for c in chunks for h in range(8)]
    pos = {c: n for n, c in enumerate(chunks)}

    def chunk_prologue(c):
        pi_t, b_pi = pin[pos[c] % NCB]
        qg_t, b_qg = qg[pos[c] % NCB]
        qT_t, b_qT = qT[pos[c] % NCB]
        y_t, b_yt = yst[pos[c] % NCB]
        S.dma("sp", pi_t[:], P[c * 128:(c + 1) * 128, 0:1536], writes=[b_pi])
        S.dma("sp", qg_t[:, 0, :], P[c * 128:(c + 1) * 128, 1536:2560], writes=[b_qg])
        S.dma("sp", qg_t[:, 1, :], P[c * 128:(c + 1) * 128, 3072:4096], writes=[b_qg])
        for h in range(4):
            S.op("pe", lambda E, h=h: E.matmul(out=pa[:, h * 128:(h + 1) * 128], lhsT=wsb[:, h, :],
                                               rhs=pi_t[:, 512 + h * 128:512 + (h + 1) * 128], start=True, stop=True),
                 reads=[b_wsb, b_pi], writes=[b_pa], signal=(h == 3))
        for h in range(4):
            S.op("dve", lambda E, h=h: E.scalar_tensor_tensor(
                out=ta[:, h * 128:(h + 1) * 128], in0=pa[:, h * 128:(h + 1) * 128], scalar=bst[:, h:h + 1],
                in1=pi_t[:, h * 128:(h + 1) * 128], op0=ALU.add, op1=ALU.mult),
                reads=[b_pa, b_bst, b_pi], writes=[b_ta])
        S.op("dve", lambda E: E.tensor_tensor(out=y_t[:, 0:512], in0=ta[:], in1=pi_t[:, 1024:1536], op=ALU.mult),
             reads=[b_ta, b_pi], writes=[b_yt])
        for g4 in range(2):
            p, b_p = ptr[g4]
            for jj in range(4):
                h = 4 * g4 + jj
                S.op("pe", lambda E, h=h, jj=jj, p=p: E.transpose(out=p[:, jj, :], in_=qg_t[:, 0, h * 128:(h + 1) * 128],
                                                                 identity=idn[:]),
                     reads=[b_qg, b_idn], writes=[b_p], signal=(jj == 3))
            S.op("act", lambda E, g4=g4, p=p: E.activation(out=qT_t[:, 4 * g4:4 * g4 + 4, :], in_=p, func=AF.Copy),
                 reads=[b_p], writes=[b_qT])

    def env(i):
        c, h = items[i]
        return c, h, c < 16, h // 4

    def st0(i):
        c, h, lat, g = env(i)
        if h == 0:
            chunk_prologue(c)
        qT_t, b_qT = qT[pos[c] % NCB]
        c_p, b_cp = psC[i % 2]
        l_p, b_lp = psL[i % 2]
        S.op("pe", lambda E: E.matmul(out=c_p, lhsT=qT_t[:, h, :], rhs=kT[:, g, 0:256], start=True, stop=True),
             reads=[b_qT, b_kT], writes=[b_cp])
        if lat:
            S.op("pe", lambda E: E.matmul(out=l_p[:, 0:384], lhsT=qT_t[:, h, :],
                                          rhs=kT[:, g, (c + 2) * 128:(c + 5) * 128], start=True, stop=True),
                 reads=[b_qT, b_kT] + ([b_kTH] if c in (0, 15) else []), writes=[b_lp])

    def st1(i):
        c, h, lat, g = env(i)
        c_p, b_cp = psC[i % 2]
        l_p, b_lp = psL[i % 2]
        S_t, b_S = Ss[i % NS_]
        mi = 0 if c == 0 else (2 if c == 15 else 1)
        if lat:
            S.op("dve", lambda E: E.tensor_tensor(out=S_t[:, 256:640], in0=l_p[:, 0:384], in1=mk[:, mi, :], op=ALU.add),
                 reads=[b_lp, b_mk], writes=[b_S])
        S.op("act", lambda E: E.activation(out=S_t[:, 0:256], in_=c_p, func=AF.Copy), reads=[b_cp], writes=[b_S])

    def st2(i):
        c, h, lat, g = env(i)
        S_t, b_S = Ss[i % NS_]
        m_t, b_m = sm[i % NM_]
        nk = 640 if lat else 256
        S.op("dve", lambda E: E.tensor_reduce(out=m_t[:, 0:1], in_=S_t[:, 0:nk], axis=AX.X, op=ALU.max),
             reads=[b_S], writes=[b_m])

    def st3(i):
        c, h, lat, g = env(i)
        m_t, b_m = sm[i % NM_]
        S.op("dve", lambda E: E.tensor_scalar(out=m_t[:, 1:2], in0=m_t[:, 0:1], scalar1=SCALE, scalar2=skB[:, h:h + 1],
                                               op0=ALU.mult, op1=ALU.max), reads=[b_m, b_sk], writes=[b_m])
        S.op("dve", lambda E: E.tensor_scalar(out=m_t[:, 2:3], in0=m_t[:, 1:2], scalar1=-1.0, scalar2=None,
                                               op0=ALU.mult), reads=[b_m], writes=[b_m])

    def st4(i):
        c, h, lat, g = env(i)
        S_t, b_S = Ss[i % NS_]
        P_t, b_Pt = Pb[i % NP_]
        m_t, b_m = sm[i % NM_]
        nk = 640 if lat else 256
        S.op("act", lambda E: E.activation(out=P_t[:, 0:nk], in_=S_t[:, 0:nk], func=AF.Exp, bias=m_t[:, 2:3],
                                           scale=SCALE, accum_out=m_t[:, 3:4]),
             reads=[b_S, b_m], writes=[b_Pt, b_m])
        S.op("act", lambda E: E.activation(out=m_t[:, 4:5], in_=skB[:, h:h + 1], func=AF.Exp, bias=m_t[:, 2:3],
                                           scale=1.0), reads=[b_sk, b_m], writes=[b_m])

    def st5(i):
        c, h, lat, g = env(i)
        P_t, b_Pt = Pb[i % NP_]
        pt_p, b_ptp = pPT[i % 2]
        nj = 5 if lat else 2
        for jj in range(nj):
            S.op("pe", lambda E, jj=jj: E.transpose(out=pt_p[:, jj, :], in_=P_t[:, jj * 128:(jj + 1) * 128],
                                                    identity=idn[:]),
                 reads=[b_Pt, b_idn], writes=[b_ptp], signal=(jj == nj - 1))

    def st6(i):
        c, h, lat, g = env(i)
        pt_p, b_ptp = pPT[i % 2]
        T_t, b_T = PT[i % NT_]
        m_t, b_m = sm[i % NM_]
        nj = 5 if lat else 2
        if i % 2 == 0:
            S.op("dve", lambda E: E.tensor_copy(out=T_t[:, 0:nj, :], in_=pt_p[:, 0:nj, :]), reads=[b_ptp], writes=[b_T])
        else:
            S.op("act", lambda E: E.activation(out=T_t[:, 0:nj, :], in_=pt_p[:, 0:nj, :], func=AF.Copy),
                 reads=[b_ptp], writes=[b_T])
        S.op("dve", lambda E: E.tensor_tensor(out=m_t[:, 5:6], in0=m_t[:, 3:4], in1=m_t[:, 4:5], op=ALU.add),
             reads=[b_m], writes=[b_m])

    def st7(i):
        c, h, lat, g = env(i)
        T_t, b_T = PT[i % NT_]
        o_p, b_op = pO[i % 2]
        m_t, b_m = sm[i % NM_]
        S.op("dve", lambda E: E.reciprocal(out=m_t[:, 6:7], in_=m_t[:, 5:6]), reads=[b_m], writes=[b_m])
        slots = [0, 1] + ([c + 2, c + 3, c + 4] if lat else [])
        for jj, sl in enumerate(slots):
            S.op("pe", lambda E, jj=jj, sl=sl: E.matmul(out=o_p, lhsT=T_t[:, jj, :],
                                                        rhs=kv[:, sl, 256 + g * 128:256 + (g + 1) * 128],
                                                        start=(jj == 0), stop=(jj == len(slots) - 1)),
                 reads=[b_T, b_kv] + ([b_kvH] if c in (0, 15) else []), writes=[b_op], signal=(jj == len(slots) - 1))

    def st8(i):
        c, h, lat, g = env(i)
        qg_t, b_qg = qg[pos[c] % NCB]
        y_t, b_yt = yst[pos[c] % NCB]
        o_p, b_op = pO[i % 2]
        m_t, b_m = sm[i % NM_]
        S.op("dve", lambda E: E.scalar_tensor_tensor(out=y_t[:, 512 + h * 128:512 + (h + 1) * 128], in0=o_p,
                                                     scalar=m_t[:, 6:7], in1=qg_t[:, 1, h * 128:(h + 1) * 128],
                                                     op0=ALU.mult, op1=ALU.mult),
             reads=[b_op, b_m, b_qg], writes=[b_yt])
        if h == 7:
            S.dma("sp", yab[c * 128:(c + 1) * 128, :], y_t[:], reads=[b_yt], writes=[b_y])

    stages = [st0, st1, st2, st3, st4, st5, st6, st7, st8]
    NST = len(stages)
    for k in range(len(items) + NST - 1):
        for s_ in range(NST - 1, -1, -1):
            i = k - s_
            if 0 <= i < len(items):
                stages[s_](i)


def const_f128():
    n = np.arange(128)
    ang = 2 * np.pi * np.outer(n, n) / 128.0
    c, s = np.cos(ang) / math.sqrt(128.0), np.sin(ang) / math.sqrt(128.0)
    return _bf(np.stack([c, -s, -c], axis=1))


def const_G():
    tb = np.arange(64)[:, None, None]
    ka = np.arange(128)[None, :, None]
    kb = np.arange(64)[None, None, :]
    ph = 2 * np.pi * ((tb * (ka + 128 * kb)) % 8192) / 8192.0
    g = np.stack([np.cos(ph), np.sin(ph)], axis=1) / 8.0
    return _bf(g.reshape(128, 128, 64))


def const_cs256():
    t = np.arange(256)[:, None]
    k = np.arange(256)[None, :]
    ang = 2 * np.pi * ((t * k) % 256) / 256.0
    m = np.stack([np.cos(ang), -np.sin(ang)], axis=1) / 16.0
    return _bf(m.reshape(2, 128, 2, 256).transpose(1, 0, 2, 3))


def ph_B(C, t):
    S = C.S
    f128, Gd, yc, bd = t.f128, t.G, t.ycB, t.bd
    b_bd, b_yc = Buf("bd"), Buf("yc")
    Zin, b_Z = C.sb([128, 64, 256], BF16, "Zin")
    F, b_F = C.sb([128, 3, 128], BF16, "F")
    G, b_G = C.sb([128, 128, 64], BF16, "G")
    Bs, b_Bs = C.sb([128, 64, 2, 128], BF16, "Bs")
    Bt, b_Bt = C.sb([128, 128, 128], BF16, "Bt")
    S.dma("sp", F[:], f128, writes=[b_F])
    b_uvs = Buf("UVsel")
    S.dma_dyn("sp", lambda E: (t.UVsel, t.UVg[bass.ds(E.JG, 8192), :]), reads=[t.b_UVg], writes=[b_uvs])
    uvv = t.UVsel.rearrange("(ta tb) c -> ta tb c", tb=64)
    for q in range(4):
        S.dma("sp" if q % 2 == 0 else "act", Zin[:, 16 * q:16 * q + 16, :], uvv[:, 16 * q:16 * q + 16, :],
              reads=[b_uvs], writes=[b_Z])
    S.dma("sp", G[:], Gd, writes=[b_G])
    par = [C.ps([128, 512], F32, f"par{i}") for i in range(2)]
    pai = [C.ps([128, 512], F32, f"pai{i}") for i in range(2)]
    pz = [C.ps([128, 4, 128], F32, f"pz{i}") for i in range(2)]
    for gq in range(16):
        a_r, b_ar = par[gq % 2]
        a_i, b_ai = pai[gq % 2]
        U = Zin[:, 4 * gq:4 * gq + 4, 0:128]
        V = Zin[:, 4 * gq:4 * gq + 4, 128:256]
        S.op("pe", lambda E, a_r=a_r, U=U: E.matmul(out=a_r[:], lhsT=F[:, 0, :], rhs=U, start=True, stop=False),
             reads=[b_F, b_Z], writes=[b_ar], signal=False)
        S.op("pe", lambda E, a_r=a_r, V=V: E.matmul(out=a_r[:], lhsT=F[:, 1, :], rhs=V, start=False, stop=True),
             reads=[b_F, b_Z], writes=[b_ar])
        S.op("pe", lambda E, a_i=a_i, U=U: E.matmul(out=a_i[:], lhsT=F[:, 1, :], rhs=U, start=True, stop=False),
             reads=[b_F, b_Z], writes=[b_ai], signal=False)
        S.op("pe", lambda E, a_i=a_i, V=V: E.matmul(out=a_i[:], lhsT=F[:, 2, :], rhs=V, start=False, stop=True),
             reads=[b_F, b_Z], writes=[b_ai])
        S.op("act", lambda E, a_r=a_r, gq=gq: E.activation(out=Bs[:, 4 * gq:4 * gq + 4, 0, :],
                                                           in_=a_r[:].rearrange("p (t d) -> p t d", d=128), func=AF.Copy),
             reads=[b_ar], writes=[b_Bs])
        S.op("dve", lambda E, a_i=a_i, gq=gq: E.tensor_copy(out=Bs[:, 4 * gq:4 * gq + 4, 1, :],
                                                            in_=a_i[:].rearrange("p (t d) -> p t d", d=128)),
             reads=[b_ai], writes=[b_Bs])
    for q in range(4):
        S.dma("sp", bd[:, 16 * q:16 * q + 16, :, :], Bs[:, 16 * q:16 * q + 16, :, :], reads=[b_Bs], writes=[b_bd])
    bdv = bd.rearrange("ka tb ri d -> (tb ri) ka d")
    for q in range(4):
        S.dma("sp" if q % 2 == 0 else "act", Bt[:, 32 * q:32 * q + 32, :], bdv[:, 32 * q:32 * q + 32, :],
              reads=[b_bd], writes=[b_Bt])
    zs = [C.sb([64, 32, 128], F32, f"zs{i}") for i in range(2)]
    ycv = yc.rearrange("(kb ka) d -> kb ka d", ka=128)
    for q in range(4):
        z_t, b_z = zs[q % 2]
        for k4 in range(8):
            p, b_p = pz[k4 % 2]
            for jj in range(4):
                ka = 32 * q + 4 * k4 + jj
                S.op("pe", lambda E, ka=ka, jj=jj, p=p: E.matmul(out=p[0:64, jj, :], lhsT=G[:, ka, :], rhs=Bt[:, ka, :],
                                                                start=True, stop=True),
                     reads=[b_G, b_Bt], writes=[b_p], signal=(jj == 3))
            if k4 % 2 == 0:
                S.op("dve", lambda E, p=p, z_t=z_t, k4=k4: E.tensor_copy(out=z_t[:, 4 * k4:4 * k4 + 4, :], in_=p[0:64, :, :]),
                     reads=[b_p], writes=[b_z])
            else:
                S.op("act", lambda E, p=p, z_t=z_t, k4=k4: E.activation(out=z_t[:, 4 * k4:4 * k4 + 4, :], in_=p[0:64, :, :],
                                                                        func=AF.Copy),
                     reads=[b_p], writes=[b_z])
        S.dma("sp", ycv[:, 32 * q:32 * q + 32, :], z_t[:], reads=[b_z], writes=[b_yc])


def ph_C(C, t, l, xin, xout, last):
    S = C.S
    yab, uvc, P, wout, gpost, bf = t.yab, t.UVc, t.P, t.wout[l], t.gpost[l:l + 1], t.bf[l:l + 1]
    cs256, ident = t.cs256, t.ident
    b_xo = Buf("xout")
    ycG3 = t.ycG.rearrange("(g n) d -> g n d", g=4)
    ycL3 = t.ycL.rearrange("(g n) d -> g n d", g=4)
    b_ycL = Buf("ycL")

    idn, b_idn = C.sb([128, 128], BF16, "idn")
    if getattr(t, "wo", None) is not None:
        wo, b_wo = t.wo
        t.wo = None
        preloaded = True
    else:
        wo, b_wo = C.sb([128, 16, D], BF16, "wo")
        preloaded = False
    ggBs = [C.sb([128, D], F32, f"ggB{i}") for i in range(2)]
    bfB, b_bf = C.sb([128, 512], F32, "bfB")
    c256, b_c256 = C.sb([128, 2, 2, 256], BF16, "c256")
    uvt, b_uvt = C.sb([128, 2, 4, 256], BF16, "uvt")
    ycc, b_ycc = C.sb([128, 2, 512], F32, "ycc")
    o1, b_o1 = C.sb([128, D], F32, "o1")
    S.dma("sp", idn[:], ident, writes=[b_idn])
    S.dma("sp", bfB[:], bf[0, :].partition_broadcast(128), writes=[b_bf])
    S.dma("sp", c256[:], cs256, writes=[b_c256])
    S.dma("sp", uvt[:], uvc.rearrange("(tc p) g c -> p tc g c", p=128), writes=[b_uvt])
    woutv = wout.rearrange("(k p) n -> p k n", p=128)
    if not preloaded:
        for q in range(8):
            S.dma("pool", wo[:, 2 * q:2 * q + 2, :], woutv[:, 2 * q:2 * q + 2, :], writes=[b_wo])
    S.dma_dyn("sp", lambda E: (t.ycL, t.ycG[bass.ds(E.JG, 8192), :]), reads=[t.b_ycG], writes=[b_ycL])

    pz = [C.ps([128, 512], F32, f"pz{i}") for i in range(4)]
    _pb = [C.ps([128, 512], F32, f"ptrb{i}") for i in range(2)]
    ptr = [(_pb[i][0][:].bitcast(BF16)[:, 0:512].rearrange("p (j d) -> p j d", d=128), _pb[i][1]) for i in range(2)]

    def load_gg(i):
        ggB, b_gg = ggBs[i]
        S.dma("sp", o1[:], gpost[0, :].partition_broadcast(128), writes=[b_o1])
        S.dma("sp", ggB[:].rearrange("p (j c) -> p j c", c=512), t.moG5[:, i, l, 2, :].partition_broadcast(128),
              writes=[b_gg])
        S.op("dve", lambda E: E.tensor_tensor(out=ggB[:], in0=ggB[:], in1=o1[:], op=ALU.mult),
             reads=[b_gg, b_o1], writes=[b_gg])

    load_gg(0)
    if not last:
        load_gg(1)

    for kc in range(2):
        p, b_p = pz[kc]
        n = 0
        for tc in range(2):
            for ri in range(2):
                S.op("pe", lambda E, kc=kc, tc=tc, ri=ri, p=p, n=n: E.matmul(
                    out=p[:], lhsT=c256[:, tc, ri, kc * 128:(kc + 1) * 128], rhs=uvt[:, tc, :, ri * 128:(ri + 1) * 128],
                    start=(n == 0), stop=(n == 3)),
                    reads=[b_c256, b_uvt], writes=[b_p], signal=(n == 3))
                n += 1
        S.op("dve", lambda E, kc=kc, p=p: E.tensor_copy(out=ycc[:, kc, :], in_=p[:]), reads=[b_p], writes=[b_ycc])

    xt = [C.sb([128, D], F32, f"xt{i}") for i in range(2)]
    ysb = [C.sb([128, D], BF16, f"ysb{i}") for i in range(2)]
    yct = [C.sb([128, 512], F32, f"yct{i}") for i in range(2)]
    gct = [C.sb([128, 512], BF16, f"gct{i}") for i in range(2)]
    yT = [C.sb([128, 16, 128], BF16, f"yT{i}") for i in range(2)]
    xo = [C.sb([128, D], F32, f"xo{i}") for i in range(2)]
    t1, b_t1 = C.sb([128, 512], F32, "t1")
    jk, b_jk = C.sb([128, 512], BF16, "jk")
    ss = [C.sb([128, 8], F32, f"ss{i}") for i in range(2)]

    zs = [C.sb([128, D], F32, f"zs{i}") for i in range(2)]
    order = ([] if last else [16, 17]) + list(range(16))

    def stA(n):
        c = order[n]
        x_t, b_x = xt[n % 2]
        y_t, b_yt = ysb[n % 2]
        yc_t, b_yct = yct[n % 2]
        g_t, b_gt = gct[n % 2]
        rows = slice(c * 128, (c + 1) * 128)
        S.dma("sp", x_t[:], xin[rows, :], writes=[b_x])
        S.dma("sp", y_t[:, 0:1536], yab[rows, :], writes=[b_yt])
        S.dma("sp", g_t[:], P[rows, 4608:5120], writes=[b_gt])
        if c < 16:
            S.dma("sp", yc_t[:].rearrange("p (g d) -> p g d", d=128),
                  ycL3[:, c * 128:(c + 1) * 128, :].rearrange("g p d -> p g d"), reads=[b_ycL], writes=[b_yct])
            src, b_src = yc_t[:], b_yct
        else:
            src, b_src = ycc[:, c - 16, :], b_ycc
        S.op("dve", lambda E: E.tensor_tensor(out=t1[:], in0=src, in1=bfB[:], op=ALU.add),
             reads=[b_src, b_bf], writes=[b_t1])
        S.op("dve", lambda E: E.tensor_tensor(out=y_t[:, 1536:2048], in0=t1[:], in1=g_t[:], op=ALU.mult),
             reads=[b_t1, b_gt], writes=[b_yt])

    def stB(n):
        y_t, b_yt = ysb[n % 2]
        T_t, b_T = yT[n % 2]
        for g4 in range(4):
            p, b_p = ptr[g4 % 2]
            for jj in range(4):
                k = 4 * g4 + jj
                S.op("pe", lambda E, k=k, jj=jj, p=p: E.transpose(out=p[:, jj, :], in_=y_t[:, k * 128:(k + 1) * 128],
                                                                 identity=idn[:]),
                     reads=[b_yt, b_idn], writes=[b_p], signal=(jj == 3))
            if g4 % 2 == 0:
                S.op("dve", lambda E, g4=g4, p=p: E.tensor_copy(out=T_t[:, 4 * g4:4 * g4 + 4, :], in_=p[:]),
                     reads=[b_p], writes=[b_T])
            else:
                S.op("act", lambda E, g4=g4, p=p: E.activation(out=T_t[:, 4 * g4:4 * g4 + 4, :], in_=p[:], func=AF.Copy),
                     reads=[b_p], writes=[b_T])

    def stC(n):
        T_t, b_T = yT[n % 2]
        z_t, b_z = zs[n % 2]
        s_t, b_s = ss[n % 2]
        for nb in range(4):
            p, b_p = pz[nb]
            cs_ = slice(nb * 512, (nb + 1) * 512)
            for k in range(16):
                S.op("pe", lambda E, k=k, nb=nb, p=p: E.matmul(out=p[:], lhsT=T_t[:, k, :],
                                                               rhs=wo[:, k, nb * 512:(nb + 1) * 512],
                                                               start=(k == 0), stop=(k == 15)),
                     reads=[b_T, b_wo], writes=[b_p], signal=(k == 15))
            S.op("dve", lambda E, p=p, cs_=cs_: E.tensor_copy(out=z_t[:, cs_], in_=p[:]), reads=[b_p], writes=[b_z])
            S.op("act", lambda E, nb=nb, cs_=cs_: E.activation(out=jk[:], in_=z_t[:, cs_], func=AF.Square,
                                                               accum_out=s_t[:, nb:nb + 1]),
                 reads=[b_z], writes=[b_jk, b_s])

    def stD(n):
        c = order[n]
        x_t, b_x = xt[n % 2]
        z_t, b_z = zs[n % 2]
        s_t, b_s = ss[n % 2]
        xo_t, b_xot = xo[n % 2]
        rows = slice(c * 128, (c + 1) * 128)
        S.op("dve", lambda E: E.tensor_reduce(out=s_t[:, 4:5], in_=s_t[:, 0:4], axis=AX.X, op=ALU.add),
             reads=[b_s], writes=[b_s])
        S.op("act", lambda E: E.activation(out=s_t[:, 5:6], in_=s_t[:, 4:5], func=AF.Sqrt, bias=EPS, scale=1.0 / D),
             reads=[b_s], writes=[b_s])
        S.op("dve", lambda E: E.reciprocal(out=s_t[:, 6:7], in_=s_t[:, 5:6]), reads=[b_s], writes=[b_s])
        ggB, b_gg = ggBs[0 if c < 16 else 1]
        S.op("dve", lambda E: E.scalar_tensor_tensor(out=o1[:], in0=z_t[:], scalar=s_t[:, 6:7], in1=ggB[:],
                                                     op0=ALU.mult, op1=ALU.mult),
             reads=[b_z, b_s, b_gg], writes=[b_o1])
        S.op("pool", lambda E: E.tensor_tensor(out=xo_t[:, 0:1024], in0=o1[:, 0:1024], in1=x_t[:, 0:1024], op=ALU.add),
             reads=[b_o1, b_x], writes=[b_xot])
        S.op("dve", lambda E: E.tensor_tensor(out=xo_t[:, 1024:D], in0=o1[:, 1024:D], in1=x_t[:, 1024:D], op=ALU.add),
             reads=[b_o1, b_x], writes=[b_xot])
        S.dma("sp", xout[rows, :], xo_t[:], reads=[b_xot], writes=[b_xo])

    N = len(order)
    stA(0)
    stB(0)
    for n in range(N):
        if n + 1 < N:
            stA(n + 1)
        stC(n)
        if n + 1 < N:
            stB(n + 1)
        stD(n)
    return b_xo


class _T:
    pass


GROUPS = [[0, 1, 2, 3], [4, 5, 6, 7]]


def build_fused(stop=None, dbg=None, nl=4):
    nc = new_nc()
    C = Ctx(nc)
    S = C.S
    t = _T()
    t.xin = C.din("xin", [NTOK, D], F32)
    t.cT = C.din("cT", [128, 16, 2], F32)
    t.wm = C.din("wm", [4, D, 1536], F32)
    t.bm = C.din("bm", [2, 6144], F32)
    t.gpre = C.din("gpre", [4, D], F32)
    t.gpost = C.din("gpost", [4, D], F32)
    t.win = C.din("win", [4, D, INC], F32)
    t.wout = C.din("wout", [4, D, D], F32)
    t.gsgu = C.din("gsgu", [4, 512], F32)
    t.wf = C.din("wf", [4, 128, 4, 128], F32)
    t.wsT = C.din("wsT", [4, 128, 4, 128], F32)
    t.bsT = C.din("bsT", [4, 128, 4], F32)
    t.sink = C.din("sink", [4, 8], F32)
    t.bf = C.din("bf", [4, 512], F32)
    t.masks = C.din("masks", [3, 128, 384], F32)
    t.ident = C.din("ident", [128, 128], BF16)
    t.cs128 = C.din("cs128", [128, 256], BF16)
    t.cosT = C.din("cosT", [2048, 128], F32)
    t.sinS = C.din("sinS", [2048, 128], F32)
    t.f128 = C.din("f128", [128, 3, 128], BF16)
    t.G = C.din("G", [128, 128, 64], BF16)
    t.cs256 = C.din("cs256", [128, 2, 2, 256], BF16)
    t.xout = C.dout("xout", [2048, D], F32)
    xsA, _ = C.dscr("xsA", [NTOK, D], F32)
    xsB, _ = C.dscr("xsB", [NTOK, D], F32)
    t.xsA, t.xsB = xsA, xsB
    t.P, _ = C.dscr("P", [NTOK, INC], BF16)
    t.UVl, _ = C.dscr("UVl", [4 * 2048, 256], BF16)
    t.UVl3 = t.UVl.rearrange("(g n) c -> g n c", g=4)
    t.UVg, b_UVg = C.dscr("UVg", [16 * 2048, 256], BF16)
    t.b_UVg = b_UVg
    t.UVc, _ = C.dscr("UVc", [256, 4, 256], BF16)
    t.kvE, _ = C.dscr("kvE", [256, 512], BF16)
    t.kvG, b_kvG = C.dscr("kvG", [1024, 512], BF16)
    t.b_kvG = b_kvG
    t.ycB, _ = C.dscr("ycB", [8192, 128], F32)
    t.ycG, b_ycG = C.dscr("ycG", [4 * 8192, 128], F32)
    t.b_ycG = b_ycG
    t.ycL, _ = C.dscr("ycL", [4 * 2048, 128], F32)
    t.UVsel, _ = C.dscr("UVsel", [8192, 256], BF16)
    t.mo, _ = C.dscr("mo", [2, 6144], F32)
    t.moG, b_moG = C.dscr("moG", [8, 6144], F32)
    t.moG5 = t.moG.rearrange("(j r) (l v c) -> j r l v c", r=2, l=4, v=3)
    t.yab, _ = C.dscr("yab", [NTOK, 1536], BF16)
    t.bd, _ = C.dscr("bd", [128, 64, 2, 128], BF16)

    def finish(b_out):
        if dbg is not None:
            src = getattr(t, dbg)
            o = C.dout("dbg", list(src.shape), src.dtype)
            b_o = Buf("dbg")
            S.dma("sp", o, src, writes=[b_o])
            S.wait_all("sp", [b_o])
        elif b_out is not None:
            S.wait_all("sp", [b_out])
        S.emit()
        return nc

    ph_M(C, t)
    C.phase_end()
    if stop == "M":
        return finish(None)
    S.coll("AllGather", GROUPS, t.mo, t.moG, writes=[b_moG])
    C.phase_end()
    if stop == "MG":
        return finish(None)
    xs = [t.xin, xsA, xsA, xsA]
    b_out = None
    for l in range(nl):
        last = l == 3
        xin = xs[l]
        xout = t.xout if last else xs[l + 1]
        ph_A1(C, t, l, xin)
        C.phase_end()
        if stop == "A1" or stop == (l, "A1"):
            return finish(None)
        S.coll("AllGather", GROUPS, t.kvE, t.kvG, writes=[b_kvG])
        for g in range(4):
            S.coll("AllGather", GROUPS, t.UVl[g * 2048:(g + 1) * 2048, :], t.UVg[g * 8192:(g + 1) * 8192, :],
                   writes=[b_UVg])
        C.phase_end(skip_cc=True)
        if stop == "A1G" or stop == (l, "A1G"):
            return finish(None)
        allc = list(range(16)) + ([] if last else [16, 17])
        ph_A2(C, t, l, allc[1:8] + allc[0:1])
        C.phase_end()
        if stop == "A2" or stop == (l, "A2"):
            return finish(None)
        ph_B(C, t)
        C.phase_end()
        if stop == "B" or stop == (l, "B"):
            return finish(None)
        for jj in range(4):
            S.coll("AllGather", GROUPS, t.ycB[jj * 2048:(jj + 1) * 2048, :], t.ycG[jj * 8192:(jj + 1) * 8192, :],
                   writes=[t.b_ycG])
        C.phase_end(skip_cc=True)
        t.wo = C.sb([128, 16, D], BF16, "wo")
        woutv = t.wout[l].rearrange("(k p) n -> p k n", p=128)
        for q in range(8):
            S.dma("pool", t.wo[0][:, 2 * q:2 * q + 2, :], woutv[:, 2 * q:2 * q + 2, :], reads=[t.b_ycG],
                  writes=[t.wo[1]])
        ph_A2(C, t, l, allc[8:])
        C.phase_end(keep_first=1)
        b_out = ph_C(C, t, l, xin, xout, last)
        C.phase_end()
        if stop == "C" or stop == (l, "C"):
            return finish(None)
    return finish(b_out)


_NC = []


def kernel(x, c, ctx, c_ctx, w_mod, b_mod, g_pre, g_post, w_in, w_out, g_sgu, w_sgu, b_sgu, sink,
           w_fourier, b_fourier):
    f32 = lambda a: np.ascontiguousarray(np.asarray(a, dtype=np.float32))
    x, c, ctx, c_ctx = f32(x), f32(c), f32(ctx), f32(c_ctx)
    w_mod, b_mod, g_pre, g_post, w_in, w_out = f32(w_mod), f32(b_mod), f32(g_pre), f32(g_post), f32(w_in), f32(w_out)
    g_sgu, w_sgu, b_sgu, sink, w_fourier, b_fourier = f32(g_sgu), f32(w_sgu), f32(b_sgu), f32(sink), f32(w_fourier), f32(b_fourier)
    if not _NC:
        _NC.append(build_fused())
    nc = _NC[0]
    common = {
        "gpre": g_pre, "gpost": g_post, "win": w_in, "wout": w_out, "gsgu": g_sgu,
        "wf": np.ascontiguousarray(w_fourier.transpose(0, 2, 1, 3)),
        "wsT": np.ascontiguousarray(w_sgu.transpose(0, 3, 1, 2)),
        "bsT": np.ascontiguousarray(b_sgu.transpose(0, 2, 1)),
        "sink": sink, "bf": np.ascontiguousarray(b_fourier.reshape(4, 512)),
        "ident": const_ident(), "cs128": const_cs128(), "f128": const_f128(), "G": const_G(), "cs256": const_cs256(),
    }
    ropes = [rope_tables(j) for j in range(4)]
    wm4 = w_mod.reshape(4, D, 3, 4, 512)
    bm4 = b_mod.reshape(4, 3, 4, 512)
    ims = []
    for i in range(8):
        b, j = i // 4, i % 4
        cT = np.ascontiguousarray(np.stack([c[b], c_ctx]).T.reshape(16, 128, 2).transpose(1, 0, 2))
        d = dict(common)
        d.update({
            "xin": np.ascontiguousarray(np.concatenate([x[b, 2048 * j:2048 * (j + 1)], ctx[b]], axis=0)),
            "cT": cT,
            "wm": np.ascontiguousarray(wm4[:, :, :, j, :].reshape(4, D, 1536)),
            "bm": np.ascontiguousarray(np.broadcast_to(bm4[:, :, j, :].reshape(1, 6144), (2, 6144))),
            "masks": const_masks(j == 0, j == 3), "cosT": ropes[j][0], "sinS": ropes[j][1],
        })
        ims.append(d)
    res = run_bass_kernel_spmd(nc, ims, core_ids=list(range(8)))
    out = np.empty((2, 8192, D), dtype=np.float32)
    for i in range(8):
        b, j = i // 4, i % 4
        out[b, 2048 * j:2048 * (j + 1)] = res.results[i]["xout"]
    return out
```

```python
import math
import numpy as np
import ml_dtypes
import concourse.bass as bass
import concourse.mybir as mybir
from concourse.bass_utils import run_bass_kernel_spmd

F32 = mybir.dt.float32
BF16 = mybir.dt.bfloat16
AF = mybir.ActivationFunctionType
ALU = mybir.AluOpType
AX = mybir.AxisListType
NPBF = ml_dtypes.bfloat16

D = 2048
NCH = 18
NTOK = NCH * 128
INC = 5120
EPS = 1e-6
SCALE = 128 ** -0.5
NEG = -30000.0


class Buf:
    __slots__ = ("name", "w", "r")

    def __init__(self, name=""):
        self.name = name
        self.w = None
        self.r = {}


class Sched:
    ENGS = ("pe", "act", "dve", "pool", "sp")
    NDMA = 8

    def __init__(self, nc):
        self.nc = nc
        self.q = {e: [] for e in self.ENGS}
        self.semh = {}
        self.cnt = {}
        for e in self.ENGS:
            self.semh[e] = nc.alloc_semaphore(name=f"sem_{e}")
            self.cnt[e] = 0
        self.seen = {e: {} for e in self.ENGS}
        self.dq = {}
        for e in ("sp", "pool", "act"):
            keys = []
            for i in range(self.NDMA):
                k = f"d_{e}_{i}"
                self.semh[k] = nc.alloc_semaphore(name=k)
                self.cnt[k] = 0
                keys.append(k)
            self.dq[e] = [keys, 0]

    def _wait(self, eng, need):
        for s, v in need.items():
            if v <= 0:
                continue
            if s == "pe" and eng == "pe":
                continue
            if self.seen[eng].get(s, 0) >= v:
                continue
            self.seen[eng][s] = v
            h = self.semh[s]
            self.q[eng].append(lambda E, h=h, v=v: E.wait_ge(h, v))

    def _deps(self, reads, writes):
        need = {}

        def add(ev):
            if ev is None:
                return
            s, v = ev
            if need.get(s, 0) < v:
                need[s] = v
        for b in reads:
            add(b.w)
        for b in writes:
            add(b.w)
            for s, v in b.r.items():
                add((s, v))
        return need

    def _mark(self, ev, reads, writes):
        s, v = ev
        for b in reads:
            if b.r.get(s, 0) < v:
                b.r[s] = v
        for b in writes:
            b.w = ev
            b.r = {}

    def op(self, eng, fn, reads=(), writes=(), signal=True):
        self._wait(eng, self._deps(reads, writes))
        if signal:
            self.cnt[eng] += 1
            ev = (eng, self.cnt[eng])
            h = self.semh[eng]
            self.q[eng].append(lambda E, fn=fn, h=h: fn(E).then_inc(h, 1))
        else:
            ev = (eng, self.cnt[eng] + 1)
            self.q[eng].append(lambda E, fn=fn: fn(E))
        self._mark(ev, reads, writes)
        return ev

    def dma(self, eng, out, in_, reads=(), writes=(), **kw):
        keys, nxt = self.dq[eng]
        k = keys[nxt % self.NDMA]
        self.dq[eng][1] = nxt + 1
        need = self._deps(reads, writes)
        if self.cnt[k] > 0:
            need[k] = max(need.get(k, 0), self.cnt[k])
        self._wait(eng, need)
        self.cnt[k] += 16
        ev = (k, self.cnt[k])
        h = self.semh[k]
        self.q[eng].append(
            lambda E, out=out, in_=in_, h=h, kw=kw: E.dma_start(out=out, in_=in_, **kw).then_inc(h, 16))
        self._mark(ev, reads, writes)
        return ev

    def coll(self, kind, groups, in_ap, out_ap, reads=(), writes=()):
        k = "cc"
        if k not in self.semh:
            self.semh[k] = self.nc.alloc_semaphore(name="sem_cc")
            self.cnt[k] = 0
        need = self._deps(reads, writes)
        if self.cnt[k] > 0:
            need[k] = max(need.get(k, 0), self.cnt[k])
        self._wait("pool", need)
        self.cnt[k] += 1
        ev = (k, self.cnt[k])
        h = self.semh[k]
        self.q["pool"].append(lambda E: E.collective_compute(
            kind, ALU.bypass, replica_groups=groups, ins=[in_ap.opt()], outs=[out_ap.opt()]).then_inc(h, 1))
        self._mark(ev, reads, writes)
        return ev

    def dma_dyn(self, eng, fn, reads=(), writes=()):
        keys, nxt = self.dq[eng]
        k = keys[nxt % self.NDMA]
        self.dq[eng][1] = nxt + 1
        need = self._deps(reads, writes)
        if self.cnt[k] > 0:
            need[k] = max(need.get(k, 0), self.cnt[k])
        self._wait(eng, need)
        self.cnt[k] += 16
        ev = (k, self.cnt[k])
        h = self.semh[k]

        def emit(E, fn=fn, h=h):
            o, i = fn(self)
            E.dma_start(out=o, in_=i).then_inc(h, 16)
        self.q[eng].append(emit)
        self._mark(ev, reads, writes)
        return ev

    def barrier(self, skip_cc=False):
        need = {k: v for k, v in self.cnt.items() if v > 0 and not (skip_cc and k == "cc")}
        for e in self.ENGS:
            self._wait(e, dict(need))

    def wait_all(self, eng, bufs):
        need = {}
        for b in bufs:
            if b.w is not None:
                s, v = b.w
                need[s] = max(need.get(s, 0), v)
        self._wait(eng, need)

    def emit(self):
        nc = self.nc
        q = self.q
        with nc.Block() as block:
            @block.tensor
            def _(E):
                for f in q["pe"]:
                    f(E)

            @block.scalar
            def _(E):
                for f in q["act"]:
                    f(E)

            @block.vector
            def _(E):
                for f in q["dve"]:
                    f(E)

            @block.gpsimd
            def _(E):
                for f in q["pool"]:
                    f(E)

            @block.sync
            def _(E):
                pid = E.partition_id()
                j4 = E.snap(pid % 4, min_val=0, max_val=3)
                self.JG = E.snap(j4 * 8192, min_val=0, max_val=3 * 8192)
                self.PREV = E.snap(((j4 + 3) % 4) * 256, min_val=0, max_val=768)
                self.NEXT = E.snap(((j4 + 1) % 4) * 256, min_val=0, max_val=768)
                for f in q["sp"]:
                    f(E)


class Ctx:
    def __init__(self, nc):
        self.nc = nc
        self.S = Sched(nc)
        self.n = 0
        self.scopes = [[]]

    def push(self):
        self.scopes.append([])

    def pop(self, keep_first=0):
        sc = self.scopes.pop()
        for g in reversed(sc[keep_first:]):
            g.__exit__(None, None, None)
        return sc[:keep_first]

    def sb(self, shape, dt, name=None):
        self.n += 1
        g = self.nc.sbuf_tensor(f"{name or 't'}_{self.n}", list(shape), dt)
        t = g.__enter__()
        self.scopes[-1].append(g)
        return t, Buf(name)

    def ps(self, shape, dt=F32, name=None):
        self.n += 1
        g = self.nc.psum_tensor(f"{name or 'p'}_{self.n}", list(shape), dt)
        t = g.__enter__()
        self.scopes[-1].append(g)
        return t, Buf(name)

    def din(self, name, shape, dt):
        return self.nc.dram_tensor(name, list(shape), dt, kind="ExternalInput").ap()

    def dout(self, name, shape, dt):
        return self.nc.dram_tensor(name, list(shape), dt, kind="ExternalOutput").ap()

    def dscr(self, name, shape, dt):
        return self.nc.dram_tensor(name, list(shape), dt).ap(), Buf(name)

    def phase_end(self, skip_cc=False, keep_first=0):
        self.S.barrier(skip_cc)
        kept = self.pop(keep_first)
        self.push()
        self.scopes[-1].extend(kept)


def new_nc():
    return bass.Bass("TRN2", target_bir_lowering=False)


def ph_M(C, t):
    S = C.S
    cT, bm, mo = t.cT, t.bm, t.mo
    ct, b_ct = C.sb([128, 16, 2], F32, "ct")
    st, b_st = C.sb([128, 16, 2], F32, "st")
    bt, b_bt = C.sb([2, 6144], F32, "bt")
    ot, b_ot = C.sb([2, 6144], F32, "ot")
    wt = [C.sb([128, 16, 512], F32, f"wt{i}") for i in range(2)]
    pp = [C.ps([128, 512], F32, f"pp{i}") for i in range(2)]
    b_mo = Buf("mo")
    S.dma("sp", ct[:], cT, writes=[b_ct])
    S.dma("sp", bt[:], bm, writes=[b_bt])
    S.op("act", lambda E: E.activation(out=st[:], in_=ct[:], func=AF.Silu), reads=[b_ct], writes=[b_st])
    for nb in range(12):
        w, b_w = wt[nb % 2]
        p, b_p = pp[nb % 2]
        wv = t.wm[nb // 3].rearrange("(k p) n -> p k n", p=128)
        vb = nb % 3
        for hh in range(2):
            S.dma("sp" if hh == 0 else "act", w[:, 8 * hh:8 * hh + 8, :],
                  wv[:, 8 * hh:8 * hh + 8, vb * 512:(vb + 1) * 512], writes=[b_w])
        for k in range(16):
            S.op("pe", lambda E, k=k, w=w, p=p: E.matmul(out=p[0:2, :], lhsT=st[:, k, :], rhs=w[:, k, :],
                                                         start=(k == 0), stop=(k == 15)),
                 reads=[b_st, b_w], writes=[b_p], signal=(k == 15))
        S.op("dve", lambda E, p=p, nb=nb: E.tensor_tensor(out=ot[:, nb * 512:(nb + 1) * 512], in0=p[0:2, :],
                                                            in1=bt[:, nb * 512:(nb + 1) * 512], op=ALU.add),
             reads=[b_p, b_bt], writes=[b_ot])
    S.dma("sp", mo, ot[:], reads=[b_ot], writes=[b_mo])


TM_BLOCKS = [
    ("a_u", 0), ("a_v", 512), ("a_g", 1024), ("q0", 1536), ("q1", 2048), ("kv", 2560),
    ("g0", 3072), ("g1", 3584), ("c_g", 4608)]


def ph_A1(C, t, l, xin):
    S = C.S
    gpre, win, gsgu, wf = t.gpre[l:l + 1], t.win[l], t.gsgu[l:l + 1], t.wf[l]
    ident, cs128, cosT, sinS = t.ident, t.cs128, t.cosT, t.sinS
    P = t.P
    b_P, b_UV = Buf("P"), Buf("UV")

    idn, b_idn = C.sb([128, 128], BF16, "idn")
    cs, b_cs = C.sb([128, 256], BF16, "cs")
    wfb, b_wfb = C.sb([128, 4, 128], BF16, "wfb")
    wcs, b_wcs = C.sb([128, 4, 256], BF16, "wcs")
    gsB = [C.sb([128, D], F32, "gsB")]
    shB = [C.sb([128, D], F32, "shB")]
    gsg, b_gsg = C.sb([128, 512], F32, "gsg")
    cosA, b_cos = C.sb([128, 16, 128], F32, "cosA")
    sinA, b_sin = C.sb([128, 16, 128], F32, "sinA")
    hT, _ = C.sb([128, 16, NTOK], BF16, "hT")
    b_hT = [Buf(f"hT{c}") for c in range(NCH)]
    cxT, b_cxT = C.sb([128, 4, NTOK], BF16, "cxT")

    S.dma("sp", idn[:], ident, writes=[b_idn])
    S.dma("sp", cs[:], cs128, writes=[b_cs])
    S.dma("pool", wfb[:], wf, writes=[b_wfb])
    S.dma("sp", gsg[:], gsgu[0, :].partition_broadcast(128), writes=[b_gsg])
    S.dma("sp", cosA[:], cosT.rearrange("(c p) d -> p c d", p=128), writes=[b_cos])
    S.dma("sp", sinA[:], sinS.rearrange("(c p) d -> p c d", p=128), writes=[b_sin])

    xt = [C.sb([128, D], F32, f"xt{i}") for i in range(3)]
    tmp, b_tmp = xt[2]
    junk, b_junk = C.sb([128, 1024], BF16, "junk")
    hb = [C.sb([128, D], BF16, f"hb{i}") for i in range(2)]
    b_hB = [Buf("hB0"), Buf("hB1")]
    ssq = [C.sb([128, 4], F32, f"ssq{i}") for i in range(2)]
    _pb = [C.ps([128, 512], F32, f"ptrb{i}") for i in range(2)]
    ptr = [(_pb[i][0][:].bitcast(BF16)[:, 0:512].rearrange("p (j d) -> p j d", d=128), _pb[i][1]) for i in range(2)]
    pp = [C.ps([128, 512], F32, f"pp{i}") for i in range(3)]
    puv = [C.ps([128, 2, 256], F32, f"puv{i}") for i in range(2)]

    def load_mod(i):
        g_t, b_g = gsB[0]
        s_t, b_s = shB[0]
        S.dma("sp", tmp[:], gpre[0, :].partition_broadcast(128), writes=[b_tmp])
        S.dma("sp", g_t[:].rearrange("p (j c) -> p j c", c=512), t.moG5[:, i, l, 1, :].partition_broadcast(128),
              writes=[b_g])
        S.dma("sp", s_t[:].rearrange("p (j c) -> p j c", c=512), t.moG5[:, i, l, 0, :].partition_broadcast(128),
              writes=[b_s])
        S.op("dve", lambda E, g_t=g_t: E.scalar_tensor_tensor(out=g_t[:], in0=g_t[:], scalar=1.0, in1=tmp[:],
                                                               op0=ALU.add, op1=ALU.mult),
             reads=[b_g, b_tmp], writes=[b_g])

    for g in range(4):
        p, b_p = pp[g % 3]
        for j in range(2):
            S.op("pe", lambda E, g=g, j=j, p=p: E.matmul(out=p[:, j * 128:(j + 1) * 128],
                                                         lhsT=cs[:, j * 128:(j + 1) * 128], rhs=wfb[:, g, :],
                                                         start=True, stop=True),
                 reads=[b_cs, b_wfb], writes=[b_p], signal=(j == 1))
        S.op("dve", lambda E, g=g, p=p: E.tensor_copy(out=wcs[:, g, :], in_=p[:, 0:256]),
             reads=[b_p], writes=[b_wcs])

    wb = [C.sb([128, 16, 512], BF16, f"wb{i}") for i in range(2)]
    stg = [C.sb([128, 512], BF16, f"stg{i}") for i in range(2)]
    r1, b_r1 = C.sb([128, 512], F32, "r1")
    r2, b_r2 = C.sb([128, 512], F32, "r2")
    gv, b_gv = C.sb([128, 512], F32, "gv")
    sq2, b_sq2 = r1, b_r1
    sv, b_sv = C.sb([128, 4], F32, "sv")
    winv = win.rearrange("(k p) n -> p k n", p=128)
    blocks = TM_BLOCKS + [("c_x", 4096)]

    def load_w(bi):
        w, b_w = wb[bi % 2]
        c0 = blocks[bi][1]
        for q4 in range(4):
            S.dma("pool", w[:, 4 * q4:4 * q4 + 4, :], winv[:, 4 * q4:4 * q4 + 4, c0:c0 + 512], writes=[b_w])

    load_w(0)

    order = [16, 17] + list(range(16))

    def normA(n):
        c = order[n]
        if c == 16:
            load_mod(1)
        if c == 0:
            load_mod(0)
        x_t, b_x = xt[n % 3]
        h_t, b_h = hb[n % 2]
        b_h2 = b_hB[n % 2]
        s_t, b_s = ssq[n % 2]
        S.dma("sp", x_t[:], xin[c * 128:(c + 1) * 128, :], writes=[b_x])
        S.op("act", lambda E: E.activation(out=junk[:], in_=x_t[:, 0:1024], func=AF.Square, accum_out=s_t[:, 3:4]),
             reads=[b_x], writes=[b_junk, b_s])
        S.op("act", lambda E: E.activation(out=junk[:], in_=x_t[:, 1024:D], func=AF.Square, accum_out=s_t[:, 1:2]),
             reads=[b_x], writes=[b_junk, b_s])
        S.op("dve", lambda E: E.tensor_tensor(out=s_t[:, 0:1], in0=s_t[:, 3:4], in1=s_t[:, 1:2], op=ALU.add),
             reads=[b_s], writes=[b_s])
        S.op("act", lambda E: E.activation(out=s_t[:, 1:2], in_=s_t[:, 0:1], func=AF.Sqrt, bias=EPS, scale=1.0 / D),
             reads=[b_s], writes=[b_s])
        S.op("dve", lambda E: E.reciprocal(out=s_t[:, 2:3], in_=s_t[:, 1:2]), reads=[b_s], writes=[b_s])
        S.op("dve", lambda E: E.scalar_tensor_tensor(out=x_t[:], in0=x_t[:], scalar=s_t[:, 2:3], in1=gsB[0][0][:],
                                                     op0=ALU.mult, op1=ALU.mult),
             reads=[b_x, b_s, gsB[0][1]], writes=[b_x])
        S.op("pool", lambda E: E.tensor_tensor(out=h_t[:, 0:1024], in0=x_t[:, 0:1024], in1=shB[0][0][:, 0:1024], op=ALU.add),
             reads=[b_x, shB[0][1]], writes=[b_h])
        S.op("dve", lambda E: E.tensor_tensor(out=h_t[:, 1024:D], in0=x_t[:, 1024:D], in1=shB[0][0][:, 1024:D], op=ALU.add),
             reads=[b_x, shB[0][1]], writes=[b_h2])

    def normB(n):
        c = order[n]
        h_t, b_h = hb[n % 2]
        b_h2 = b_hB[n % 2]
        for g4 in range(4):
            p, b_p = ptr[g4 % 2]
            for j in range(4):
                k = 4 * g4 + j
                S.op("pe", lambda E, k=k, j=j, p=p: E.transpose(out=p[:, j, :], in_=h_t[:, k * 128:(k + 1) * 128],
                                                               identity=idn[:]),
                     reads=[b_h if k < 8 else b_h2, b_idn], writes=[b_p], signal=(j == 3))
            if g4 % 2 == 0:
                S.op("dve", lambda E, g4=g4, p=p: E.tensor_copy(out=hT[:, 4 * g4:4 * g4 + 4, c * 128:(c + 1) * 128], in_=p[:]),
                     reads=[b_p], writes=[b_hT[c]])
            else:
                S.op("act", lambda E, g4=g4, p=p: E.activation(out=hT[:, 4 * g4:4 * g4 + 4, c * 128:(c + 1) * 128],
                                                               in_=p[:], func=AF.Copy),
                     reads=[b_p], writes=[b_hT[c]])

    normA(0)
    for n in range(NCH):
        if n + 1 < NCH:
            normA(n + 1)
        normB(n)

    def rope(src, dst, c, ncols, bs, bd):
        nh = ncols // 128
        cb = cosA[:, c, :].unsqueeze(1).to_broadcast([128, nh, 128])
        S.op("dve", lambda E: E.tensor_tensor(out=r1[:, 0:ncols].rearrange("p (h d) -> p h d", d=128),
                                              in0=src.rearrange("p (h d) -> p h d", d=128), in1=cb, op=ALU.mult),
             reads=[bs, b_cos], writes=[b_r1])
        s5 = src.rearrange("p (h a b f) -> p h a b f", a=2, b=2, f=32)
        r5 = r2[:, 0:ncols].rearrange("p (h a b f) -> p h a b f", a=2, b=2, f=32)
        n4 = sinA[:, c, :].rearrange("p (a b f) -> p a b f", a=2, b=2, f=32)
        for bdst in range(2):
            sb_ = n4[:, :, bdst, :].unsqueeze(1).to_broadcast([128, nh, 2, 32])
            S.op("dve", lambda E, bdst=bdst, sb_=sb_: E.tensor_tensor(
                out=r5[:, :, :, bdst, :], in0=s5[:, :, :, 1 - bdst, :], in1=sb_, op=ALU.mult),
                reads=[bs, b_sin], writes=[b_r2])
        S.op("dve", lambda E: E.tensor_tensor(out=dst, in0=r1[:, 0:ncols], in1=r2[:, 0:ncols], op=ALU.add),
             reads=[b_r1, b_r2], writes=[bd])

    for bi, (bname, c0) in enumerate(blocks):
        if bi + 1 < len(blocks):
            load_w(bi + 1)
        w, b_w = wb[bi % 2]
        if bname != "c_x":
            for c in range(NCH):
                p, b_p = pp[c % 3]
                st_t, b_st = stg[c % 2]
                for k in range(16):
                    S.op("pe", lambda E, k=k, c=c, w=w, p=p: E.matmul(out=p[:], lhsT=hT[:, k, c * 128:(c + 1) * 128],
                                                                    rhs=w[:, k, :], start=(k == 0), stop=(k == 15)),
                         reads=[b_hT[c], b_w], writes=[b_p], signal=(k == 15))
                lat = c < 16
                if bname == "a_u":
                    S.op("act", lambda E, p=p, st_t=st_t: E.activation(out=st_t[:], in_=p[:], func=AF.Gelu_apprx_tanh),
                         reads=[b_p], writes=[b_st])
                elif bname == "a_v":
                    S.op("act", lambda E, p=p: E.activation(out=gv[:], in_=p[:], func=AF.Gelu_apprx_tanh),
                         reads=[b_p], writes=[b_gv])
                    S.op("act", lambda E: E.activation(out=sq2[:], in_=gv[:], func=AF.Square, accum_out=sv[:, 0:1]),
                         reads=[b_gv], writes=[b_sq2, b_sv])
                    S.op("act", lambda E: E.activation(out=sv[:, 1:2], in_=sv[:, 0:1], func=AF.Sqrt, bias=EPS,
                                                       scale=1.0 / 512),
                         reads=[b_sv], writes=[b_sv])
                    S.op("dve", lambda E: E.reciprocal(out=sv[:, 2:3], in_=sv[:, 1:2]), reads=[b_sv], writes=[b_sv])
                    S.op("dve", lambda E, st_t=st_t: E.scalar_tensor_tensor(out=st_t[:], in0=gv[:], scalar=sv[:, 2:3],
                                                                           in1=gsg[:], op0=ALU.mult, op1=ALU.mult),
                         reads=[b_gv, b_sv, b_gsg], writes=[b_st])
                elif bname in ("a_g", "g0", "g1", "c_g"):
                    S.op("act", lambda E, p=p, st_t=st_t: E.activation(out=st_t[:], in_=p[:], func=AF.Silu),
                         reads=[b_p], writes=[b_st])
                elif bname in ("q0", "q1"):
                    if lat:
                        rope(p[:], st_t[:], c, 512, b_p, b_st)
                    else:
                        S.op("act", lambda E, p=p, st_t=st_t: E.activation(out=st_t[:], in_=p[:], func=AF.Copy),
                             reads=[b_p], writes=[b_st])
                elif bname == "kv":
                    if lat:
                        rope(p[:, 0:256], st_t[:, 0:256], c, 256, b_p, b_st)
                        S.op("act", lambda E, p=p, st_t=st_t: E.activation(out=st_t[:, 256:512], in_=p[:, 256:512],
                                                                           func=AF.Copy),
                             reads=[b_p], writes=[b_st])
                    else:
                        S.op("act", lambda E, p=p, st_t=st_t: E.activation(out=st_t[:], in_=p[:], func=AF.Copy),
                             reads=[b_p], writes=[b_st])
                S.dma("sp", P[c * 128:(c + 1) * 128, c0:c0 + 512], st_t[:], reads=[b_st], writes=[b_P])
                if bname == "kv" and c in (0, 15):
                    e0 = 0 if c == 0 else 128
                    S.dma("sp", t.kvE[e0:e0 + 128, :], st_t[:], reads=[b_st], writes=[b_UV])
        else:
            tblocks = [(0, 512), (512, 512), (1024, 512), (1536, 512), (2048, 256)]
            it = 0
            for g in range(4):
                for (t0, tn) in tblocks:
                    p, b_p = pp[it % 3]
                    it += 1
                    cl = sorted(set(range(t0 // 128, (t0 + tn) // 128)))
                    for k in range(16):
                        S.op("pe", lambda E, k=k, g=g, t0=t0, tn=tn, w=w, p=p: E.matmul(
                            out=p[:, 0:tn], lhsT=w[:, k, g * 128:(g + 1) * 128], rhs=hT[:, k, t0:t0 + tn],
                            start=(k == 0), stop=(k == 15)),
                            reads=[b_hT[c] for c in cl] + [b_w], writes=[b_p], signal=(k == 15))
                    if it % 2 == 0:
                        S.op("dve", lambda E, g=g, t0=t0, tn=tn, p=p: E.tensor_copy(out=cxT[:, g, t0:t0 + tn],
                                                                                  in_=p[:, 0:tn]),
                             reads=[b_p], writes=[b_cxT])
                    else:
                        S.op("act", lambda E, g=g, t0=t0, tn=tn, p=p: E.activation(out=cxT[:, g, t0:t0 + tn],
                                                                                 in_=p[:, 0:tn], func=AF.Copy),
                             reads=[b_p], writes=[b_cxT])
            uvs = [C.sb([128, 4, 256], BF16, f"uvs{i}") for i in range(2)]
            for c in range(NCH):
                u_t, b_u = uvs[c % 2]
                for half in range(2):
                    p, b_p = puv[half]
                    for gg in range(2):
                        g = 2 * half + gg
                        S.op("pe", lambda E, g=g, gg=gg, c=c, p=p: E.matmul(
                            out=p[:, gg, :], lhsT=cxT[:, g, c * 128:(c + 1) * 128], rhs=wcs[:, g, :],
                            start=True, stop=True),
                            reads=[b_cxT, b_wcs], writes=[b_p], signal=(gg == 1))
                    if half == 0:
                        S.op("dve", lambda E, p=p, u_t=u_t: E.tensor_copy(out=u_t[:, 0:2, :], in_=p[:]),
                             reads=[b_p], writes=[b_u])
                    else:
                        S.op("act", lambda E, p=p, u_t=u_t: E.activation(out=u_t[:, 2:4, :], in_=p[:], func=AF.Copy),
                             reads=[b_p], writes=[b_u])
                if c < 16:
                    S.dma("sp", t.UVl3[:, c * 128:(c + 1) * 128, :].rearrange("g p c -> p g c"), u_t[:],
                          reads=[b_u], writes=[b_UV])
                else:
                    S.dma("sp", t.UVc[(c - 16) * 128:(c - 15) * 128, :, :], u_t[:], reads=[b_u], writes=[b_UV])


def _bf(a):
    return np.ascontiguousarray(np.asarray(a, dtype=np.float32)).astype(NPBF)


def const_ident():
    return _bf(np.eye(128))


def const_cs128():
    n = np.arange(128)
    ang = 2 * np.pi * np.outer(n, n) / 128.0
    return _bf(np.concatenate([np.cos(ang), np.sin(ang)], axis=1) / math.sqrt(128.0))


def rope_tables(j):
    t = np.arange(2048 * j, 2048 * j + 2048)
    row = (t // 64).astype(np.float32)
    col = (t % 64).astype(np.float32)
    freqs = (np.float32(10000.0) ** (-np.arange(32, dtype=np.float32) / np.float32(32))).astype(np.float32)
    ang = np.stack([row[:, None] * freqs, col[:, None] * freqs], axis=1)
    ang = np.broadcast_to(ang[:, :, None, :], (2048, 2, 2, 32)).astype(np.float32)
    cos = np.cos(ang).astype(np.float32)
    sin = np.sin(ang).astype(np.float32).copy()
    sin[:, :, 0, :] *= -1.0
    return np.ascontiguousarray(cos.reshape(2048, 128)), np.ascontiguousarray(sin.reshape(2048, 128))


def const_masks(first, last):
    i = np.arange(128)[:, None]
    jj = np.arange(128)[None, :]
    mid = np.zeros((128, 384), np.float32)
    mid[:, 0:128] = np.where(jj >= i, 0.0, NEG)
    mid[:, 256:384] = np.where(jj <= i, 0.0, NEG)
    mf = mid.copy()
    ml = mid.copy()
    if first:
        mf[:, 0:128] = NEG
    if last:
        ml[:, 256:384] = NEG
    return np.ascontiguousarray(np.stack([mf, mid, ml]))


def ph_A2(C, t, l, chunks):
    S = C.S
    P, wsT, bsT, sink, masks, ident, yab = t.P, t.wsT[l], t.bsT[l], t.sink[l:l + 1], t.masks, t.ident, t.yab
    b_y = Buf("yab")

    idn, b_idn = C.sb([128, 128], BF16, "idn")
    wsb, b_wsb = C.sb([128, 4, 128], BF16, "wsb")
    bst, b_bst = C.sb([128, 4], F32, "bst")
    skB, b_sk = C.sb([128, 8], F32, "skB")
    mk, b_mk = C.sb([128, 3, 384], F32, "mk")
    kv, b_kv = C.sb([128, 21, 512], BF16, "kv")
    kT, b_kT = C.sb([128, 2, 21 * 128], BF16, "kT")
    S.dma("sp", idn[:], ident, writes=[b_idn])
    wsf, b_wsf = C.sb([128, 4, 128], F32, "wsf")
    S.dma("sp", wsf[:], wsT, writes=[b_wsf])
    S.op("act", lambda E: E.activation(out=wsb[:], in_=wsf[:], func=AF.Copy), reads=[b_wsf], writes=[b_wsb])
    S.dma("sp", bst[:], bsT, writes=[b_bst])
    S.dma("sp", skB[:], sink[0, :].partition_broadcast(128), writes=[b_sk])
    S.dma("sp", mk[:], masks.rearrange("m p k -> p m k"), writes=[b_mk])
    Pv = P.rearrange("(c p) n -> p c n", p=128)
    S.dma("sp", kv[:, 0:2, :], Pv[:, 16:18, 2560:3072], writes=[b_kv])
    b_kvH, b_kTH = Buf("kvH"), Buf("kTH")
    S.dma_dyn("sp", lambda E: (kv[:, 2, :], t.kvG[bass.ds(E.PREV, 256), :][128:256, :]),
              reads=[t.b_kvG], writes=[b_kvH])
    S.dma("sp", kv[:, 3:19, :], Pv[:, 0:16, 2560:3072], writes=[b_kv])
    S.dma_dyn("sp", lambda E: (kv[:, 19, :], t.kvG[bass.ds(E.NEXT, 256), :][0:128, :]),
              reads=[t.b_kvG], writes=[b_kvH])

    bk = [C.ps([128, 512], F32, f"bk{i}") for i in range(8)]

    def bfv(i):
        return bk[i][0][:].bitcast(BF16)
    psC = [(bk[0][0][:, 0:256], bk[0][1]), (bk[2][0][:, 0:256], bk[2][1])]
    psL = [(bk[1][0], bk[1][1]), (bk[3][0], bk[3][1])]
    pPT = [(bfv(4)[:, 0:640].rearrange("p (j d) -> p j d", d=128), bk[4][1]),
           (bfv(5)[:, 0:640].rearrange("p (j d) -> p j d", d=128), bk[5][1])]
    pO = [(bk[6][0][:, 0:128], bk[6][1]), (bk[6][0][:, 128:256], bk[6][1])]
    pa, b_pa = bk[7][0], bk[7][1]
    ptr = [(bfv(7)[:, 0:512].rearrange("p (j d) -> p j d", d=128), bk[7][1]),
           (bfv(7)[:, 512:1024].rearrange("p (j d) -> p j d", d=128), bk[7][1])]

    it = 0
    own = [0, 1] + list(range(3, 19))
    for g in range(2):
        for s0 in range(0, len(own), 4):
            sl = own[s0:s0 + 4]
            ns = len(sl)
            p, b_p = ptr[it % 2]
            it += 1
            for jj, sidx in enumerate(sl):
                S.op("pe", lambda E, g=g, sidx=sidx, jj=jj, p=p: E.transpose(out=p[:, jj, :],
                                                                           in_=kv[:, sidx, g * 128:(g + 1) * 128],
                                                                           identity=idn[:]),
                     reads=[b_kv, b_idn], writes=[b_p], signal=(jj == ns - 1))
            for jj, sidx in enumerate(sl):
                pass
            runs = []
            for jj, sidx in enumerate(sl):
                if runs and runs[-1][1] + runs[-1][2] == sidx:
                    runs[-1][2] += 1
                else:
                    runs.append([jj, sidx, 1])
            for (j0, sstart, n_) in runs:
                S.op("dve", lambda E, g=g, j0=j0, sstart=sstart, n_=n_, p=p: E.tensor_copy(
                    out=kT[:, g, sstart * 128:(sstart + n_) * 128].rearrange("p (s d) -> p s d", d=128),
                    in_=p[:, j0:j0 + n_, :]), reads=[b_p], writes=[b_kT])
    p, b_p = ptr[it % 2]
    for jj, (g, sidx) in enumerate([(0, 2), (1, 2), (0, 19), (1, 19)]):
        S.op("pe", lambda E, g=g, sidx=sidx, jj=jj, p=p: E.transpose(out=p[:, jj, :], in_=kv[:, sidx, g * 128:(g + 1) * 128],
                                                                   identity=idn[:]),
             reads=[b_kvH, b_idn], writes=[b_p], signal=(jj == 3))
    for jj, (g, sidx) in enumerate([(0, 2), (1, 2), (0, 19), (1, 19)]):
        S.op("act", lambda E, g=g, sidx=sidx, jj=jj, p=p: E.activation(out=kT[:, g, sidx * 128:(sidx + 1) * 128],
                                                                     in_=p[:, jj, :], func=AF.Copy),
             reads=[b_p], writes=[b_kTH])

    NCB = 3
    pin = [C.sb([128, 1536], BF16, f"pin{i}") for i in range(NCB)]
    qg = [C.sb([128, 2, 1024], BF16, f"qg{i}") for i in range(NCB)]
    qT = [C.sb([128, 8, 128], BF16, f"qT{i}") for i in range(NCB)]
    yst = [C.sb([128, 1536], BF16, f"yst{i}") for i in range(NCB)]
    ta, b_ta = C.sb([128, 512], F32, "ta")
    NS_, NP_, NT_, NM_ = 4, 4, 3, 12
    Ss = [C.sb([128, 640], F32, f"Ss{i}") for i in range(NS_)]
    Pb = [C.sb([128, 640], BF16, f"Pb{i}") for i in range(NP_)]
    PT = [C.sb([128, 5, 128], BF16, f"PT{i}") for i in range(NT_)]
    sm = [C.sb([128, 8], F32, f"sm{i}") for i in range(NM_)]

    items = [(c, h) for c in chunks for h in range(8)]
    pos = {c: n for n, c in enumerate(chunks)}

    def chunk_prologue(c):
        pi_t, b_pi = pin[pos[c] % NCB]
        qg_t, b_qg = qg[pos[c] % NCB]
        qT_t, b_qT = qT[pos[c] % NCB]
        y_t, b_yt = yst[pos[c] % NCB]
        S.dma("sp", pi_t[:], P[c * 128:(c + 1) * 128, 0:1536], writes=[b_pi])
        S.dma("sp", qg_t[:, 0, :], P[c * 128:(c + 1) * 128, 1536:2560], writes=[b_qg])
        S.dma("sp", qg_t[:, 1, :], P[c * 128:(c + 1) * 128, 3072:4096], writes=[b_qg])
        for h in range(4):
            S.op("pe", lambda E, h=h: E.matmul(out=pa[:, h * 128:(h + 1) * 128], lhsT=wsb[:, h, :],
                                               rhs=pi_t[:, 512 + h * 128:512 + (h + 1) * 128], start=True, stop=True),
                 reads=[b_wsb, b_pi], writes=[b_pa], signal=(h == 3))
        for h in range(4):
            S.op("dve", lambda E, h=h: E.scalar_tensor_tensor(
                out=ta[:, h * 128:(h + 1) * 128], in0=pa[:, h * 128:(h + 1) * 128], scalar=bst[:, h:h + 1],
                in1=pi_t[:, h * 128:(h + 1) * 128], op0=ALU.add, op1=ALU.mult),
                reads=[b_pa, b_bst, b_pi], writes=[b_ta])
        S.op("dve", lambda E: E.tensor_tensor(out=y_t[:, 0:512], in0=ta[:], in1=pi_t[:, 1024:1536], op=ALU.mult),
             reads=[b_ta, b_pi], writes=[b_yt])
        for g4 in range(2):
            p, b_p = ptr[g4]
            for jj in range(4):
                h = 4 * g4 + jj
                S.op("pe", lambda E, h=h, jj=jj, p=p: E.transpose(out=p[:, jj, :], in_=qg_t[:, 0, h * 128:(h + 1) * 128],
                                                                 identity=idn[:]),
                     reads=[b_qg, b_idn], writes=[b_p], signal=(jj == 3))
            S.op("act", lambda E, g4=g4, p=p: E.activation(out=qT_t[:, 4 * g4:4 * g4 + 4, :], in_=p, func=AF.Copy),
                 reads=[b_p], writes=[b_qT])

    def env(i):
        c, h = items[i]
        return c, h, c < 16, h // 4

    def st0(i):
        c, h, lat, g = env(i)
        if h == 0:
            chunk_prologue(c)
        qT_t, b_qT = qT[pos[c] % NCB]
        c_p, b_cp = psC[i % 2]
        l_p, b_lp = psL[i % 2]
        S.op("pe", lambda E: E.matmul(out=c_p, lhsT=qT_t[:, h, :], rhs=kT[:, g, 0:256], start=True, stop=True),
             reads=[b_qT, b_kT], writes=[b_cp])
        if lat:
            S.op("pe", lambda E: E.matmul(out=l_p[:, 0:384], lhsT=qT_t[:, h, :],
                                          rhs=kT[:, g, (c + 2) * 128:(c + 5) * 128], start=True, stop=True),
                 reads=[b_qT, b_kT] + ([b_kTH] if c in (0, 15) else []), writes=[b_lp])

    def st1(i):
        c, h, lat, g = env(i)
        c_p, b_cp = psC[i % 2]
        l_p, b_lp = psL[i % 2]
        S_t, b_S = Ss[i % NS_]
        mi = 0 if c == 0 else (2 if c == 15 else 1)
        if lat:
            S.op("dve", lambda E: E.tensor_tensor(out=S_t[:, 256:640], in0=l_p[:, 0:384], in1=mk[:, mi, :], op=ALU.add),
                 reads=[b_lp, b_mk], writes=[b_S])
        S.op("act", lambda E: E.activation(out=S_t[:, 0:256], in_=c_p, func=AF.Copy), reads=[b_cp], writes=[b_S])

    def st2(i):
        c, h, lat, g = env(i)
        S_t, b_S = Ss[i % NS_]
        m_t, b_m = sm[i % NM_]
        nk = 640 if lat else 256
        S.op("dve", lambda E: E.tensor_reduce(out=m_t[:, 0:1], in_=S_t[:, 0:nk], axis=AX.X, op=ALU.max),
             reads=[b_S], writes=[b_m])

    def st3(i):
        c, h, lat, g = env(i)
        m_t, b_m = sm[i % NM_]
        S.op("dve", lambda E: E.tensor_scalar(out=m_t[:, 1:2], in0=m_t[:, 0:1], scalar1=SCALE, scalar2=skB[:, h:h + 1],
                                               op0=ALU.mult, op1=ALU.max), reads=[b_m, b_sk], writes=[b_m])
        S.op("dve", lambda E: E.tensor_scalar(out=m_t[:, 2:3], in0=m_t[:, 1:2], scalar1=-1.0, scalar2=None,
                                               op0=ALU.mult), reads=[b_m], writes=[b_m])

    def st4(i):
        c, h, lat, g = env(i)
        S_t, b_S = Ss[i % NS_]
        P_t, b_Pt = Pb[i % NP_]
        m_t, b_m = sm[i % NM_]
        nk = 640 if lat else 256
        S.op("act", lambda E: E.activation(out=P_t[:, 0:nk], in_=S_t[:, 0:nk], func=AF.Exp, bias=m_t[:, 2:3],
                                           scale=SCALE, accum_out=m_t[:, 3:4]),
             reads=[b_S, b_m], writes=[b_Pt, b_m])
        S.op("act", lambda E: E.activation(out=m_t[:, 4:5], in_=skB[:, h:h + 1], func=AF.Exp, bias=m_t[:, 2:3],
                                           scale=1.0), reads=[b_sk, b_m], writes=[b_m])

    def st5(i):
        c, h, lat, g = env(i)
        P_t, b_Pt = Pb[i % NP_]
        pt_p, b_ptp = pPT[i % 2]
        nj = 5 if lat else 2
        for jj in range(nj):
            S.op("pe", lambda E, jj=jj: E.transpose(out=pt_p[:, jj, :], in_=P_t[:, jj * 128:(jj + 1) * 128],
                                                    identity=idn[:]),
                 reads=[b_Pt, b_idn], writes=[b_ptp], signal=(jj == nj - 1))

    def st6(i):
        c, h, lat, g = env(i)
        pt_p, b_ptp = pPT[i % 2]
        T_t, b_T = PT[i % NT_]
        m_t, b_m = sm[i % NM_]
        nj = 5 if lat else 2
        if i % 2 == 0:
            S.op("dve", lambda E: E.tensor_copy(out=T_t[:, 0:nj, :], in_=pt_p[:, 0:nj, :]), reads=[b_ptp], writes=[b_T])
        else:
            S.op("act", lambda E: E.activation(out=T_t[:, 0:nj, :], in_=pt_p[:, 0:nj, :], func=AF.Copy),
                 reads=[b_ptp], writes=[b_T])
        S.op("dve", lambda E: E.tensor_tensor(out=m_t[:, 5:6], in0=m_t[:, 3:4], in1=m_t[:, 4:5], op=ALU.add),
             reads=[b_m], writes=[b_m])

    def st7(i):
        c, h, lat, g = env(i)
        T_t, b_T = PT[i % NT_]
        o_p, b_op = pO[i % 2]
        m_t, b_m = sm[i % NM_]
        S.op("dve", lambda E: E.reciprocal(out=m_t[:, 6:7], in_=m_t[:, 5:6]), reads=[b_m], writes=[b_m])
        slots = [0, 1] + ([c + 2, c + 3, c + 4] if lat else [])
        for jj, sl in enumerate(slots):
            S.op("pe", lambda E, jj=jj, sl=sl: E.matmul(out=o_p, lhsT=T_t[:, jj, :],
                                                        rhs=kv[:, sl, 256 + g * 128:256 + (g + 1) * 128],
                                                        start=(jj == 0), stop=(jj == len(slots) - 1)),
                 reads=[b_T, b_kv] + ([b_kvH] if c in (0, 15) else []), writes=[b_op], signal=(jj == len(slots) - 1))

    def st8(i):
        c, h, lat, g = env(i)
        qg_t, b_qg = qg[pos[c] % NCB]
        y_t, b_yt = yst[pos[c] % NCB]
        o_p, b_op = pO[i % 2]
        m_t, b_m = sm[i % NM_]
        S.op("dve", lambda E: E.scalar_tensor_tensor(out=y_t[:, 512 + h * 128:512 + (h + 1) * 128], in0=o_p,
                                                     scalar=m_t[:, 6:7], in1=qg_t[:, 1, h * 128:(h + 1) * 128],
                                                     op0=ALU.mult, op1=ALU.mult),
             reads=[b_op, b_m, b_qg], writes=[b_yt])
        if h == 7:
            S.dma("sp", yab[c * 128:(c + 1) * 128, :], y_t[:], reads=[b_yt], writes=[b_y])

    stages = [st0, st1, st2, st3, st4, st5, st6, st7, st8]
    NST = len(stages)
    for k in range(len(items) + NST - 1):
        for s_ in range(NST - 1, -1, -1):
            i = k - s_
            if 0 <= i < len(items):
                stages[s_](i)


def const_f128():
    n = np.arange(128)
    ang = 2 * np.pi * np.outer(n, n) / 128.0
    c, s = np.cos(ang) / math.sqrt(128.0), np.sin(ang) / math.sqrt(128.0)
    return _bf(np.stack([c, -s, -c], axis=1))


def const_G():
    tb = np.arange(64)[:, None, None]
    ka = np.arange(128)[None, :, None]
    kb = np.arange(64)[None, None, :]
    ph = 2 * np.pi * ((tb * (ka + 128 * kb)) % 8192) / 8192.0
    g = np.stack([np.cos(ph), np.sin(ph)], axis=1) / 8.0
    return _bf(g.reshape(128, 128, 64))


def const_cs256():
    t = np.arange(256)[:, None]
    k = np.arange(256)[None, :]
    ang = 2 * np.pi * ((t * k) % 256) / 256.0
    m = np.stack([np.cos(ang), -np.sin(ang)], axis=1) / 16.0
    return _bf(m.reshape(2, 128, 2, 256).transpose(1, 0, 2, 3))


def ph_B(C, t):
    S = C.S
    f128, Gd, yc, bd = t.f128, t.G, t.ycB, t.bd
    b_bd, b_yc = Buf("bd"), Buf("yc")
    Zin, b_Z = C.sb([128, 64, 256], BF16, "Zin")
    F, b_F = C.sb([128, 3, 128], BF16, "F")
    G, b_G = C.sb([128, 128, 64], BF16, "G")
    Bs, b_Bs = C.sb([128, 64, 2, 128], BF16, "Bs")
    Bt, b_Bt = C.sb([128, 128, 128], BF16, "Bt")
    S.dma("sp", F[:], f128, writes=[b_F])
    for q in range(4):
        S.dma_dyn("sp", lambda E, q=q: (
            Zin[:, 16 * q:16 * q + 16, :],
            t.UVg[bass.ds(E.JG, 8192), :].rearrange("(ta tb) c -> ta tb c", tb=64)[:, 16 * q:16 * q + 16, :]),
            reads=[t.b_UVg], writes=[b_Z])
    S.dma("sp", G[:], Gd, writes=[b_G])
    par = [C.ps([128, 512], F32, f"par{i}") for i in range(2)]
    pai = [C.ps([128, 512], F32, f"pai{i}") for i in range(2)]
    pz = [C.ps([128, 4, 128], F32, f"pz{i}") for i in range(2)]
    for gq in range(16):
        a_r, b_ar = par[gq % 2]
        a_i, b_ai = pai[gq % 2]
        U = Zin[:, 4 * gq:4 * gq + 4, 0:128]
        V = Zin[:, 4 * gq:4 * gq + 4, 128:256]
        S.op("pe", lambda E, a_r=a_r, U=U: E.matmul(out=a_r[:], lhsT=F[:, 0, :], rhs=U, start=True, stop=False),
             reads=[b_F, b_Z], writes=[b_ar], signal=False)
        S.op("pe", lambda E, a_r=a_r, V=V: E.matmul(out=a_r[:], lhsT=F[:, 1, :], rhs=V, start=False, stop=True),
             reads=[b_F, b_Z], writes=[b_ar])
        S.op("pe", lambda E, a_i=a_i, U=U: E.matmul(out=a_i[:], lhsT=F[:, 1, :], rhs=U, start=True, stop=False),
             reads=[b_F, b_Z], writes=[b_ai], signal=False)
        S.op("pe", lambda E, a_i=a_i, V=V: E.matmul(out=a_i[:], lhsT=F[:, 2, :], rhs=V, start=False, stop=True),
             reads=[b_F, b_Z], writes=[b_ai])
        S.op("act", lambda E, a_r=a_r, gq=gq: E.activation(out=Bs[:, 4 * gq:4 * gq + 4, 0, :],
                                                           in_=a_r[:].rearrange("p (t d) -> p t d", d=128), func=AF.Copy),
             reads=[b_ar], writes=[b_Bs])
        S.op("dve", lambda E, a_i=a_i, gq=gq: E.tensor_copy(out=Bs[:, 4 * gq:4 * gq + 4, 1, :],
                                                            in_=a_i[:].rearrange("p (t d) -> p t d", d=128)),
             reads=[b_ai], writes=[b_Bs])
    for q in range(4):
        S.dma("sp", bd[:, 16 * q:16 * q + 16, :, :], Bs[:, 16 * q:16 * q + 16, :, :], reads=[b_Bs], writes=[b_bd])
    bdv = bd.rearrange("ka tb ri d -> (tb ri) ka d")
    for q in range(4):
        S.dma("sp" if q % 2 == 0 else "act", Bt[:, 32 * q:32 * q + 32, :], bdv[:, 32 * q:32 * q + 32, :],
              reads=[b_bd], writes=[b_Bt])
    zs = [C.sb([64, 32, 128], F32, f"zs{i}") for i in range(2)]
    ycv = yc.rearrange("(kb ka) d -> kb ka d", ka=128)
    for q in range(4):
        z_t, b_z = zs[q % 2]
        for k4 in range(8):
            p, b_p = pz[k4 % 2]
            for jj in range(4):
                ka = 32 * q + 4 * k4 + jj
                S.op("pe", lambda E, ka=ka, jj=jj, p=p: E.matmul(out=p[0:64, jj, :], lhsT=G[:, ka, :], rhs=Bt[:, ka, :],
                                                                start=True, stop=True),
                     reads=[b_G, b_Bt], writes=[b_p], signal=(jj == 3))
            if k4 % 2 == 0:
                S.op("dve", lambda E, p=p, z_t=z_t, k4=k4: E.tensor_copy(out=z_t[:, 4 * k4:4 * k4 + 4, :], in_=p[0:64, :, :]),
                     reads=[b_p], writes=[b_z])
            else:
                S.op("act", lambda E, p=p, z_t=z_t, k4=k4: E.activation(out=z_t[:, 4 * k4:4 * k4 + 4, :], in_=p[0:64, :, :],
                                                                        func=AF.Copy),
                     reads=[b_p], writes=[b_z])
        S.dma("sp", ycv[:, 32 * q:32 * q + 32, :], z_t[:], reads=[b_z], writes=[b_yc])


def ph_C(C, t, l, xin, xout, last):
    S = C.S
    yab, uvc, P, wout, gpost, bf = t.yab, t.UVc, t.P, t.wout[l], t.gpost[l:l + 1], t.bf[l:l + 1]
    cs256, ident = t.cs256, t.ident
    b_xo = Buf("xout")
    ycG3 = t.ycG.rearrange("(g n) d -> g n d", g=4)
    ycL3 = t.ycL.rearrange("(g n) d -> g n d", g=4)
    b_ycL = Buf("ycL")

    idn, b_idn = C.sb([128, 128], BF16, "idn")
    if getattr(t, "wo", None) is not None:
        wo, b_wo = t.wo
        t.wo = None
        preloaded = True
    else:
        wo, b_wo = C.sb([128, 16, D], BF16, "wo")
        preloaded = False
    ggBs = [C.sb([128, D], F32, f"ggB{i}") for i in range(2)]
    bfB, b_bf = C.sb([128, 512], F32, "bfB")
    c256, b_c256 = C.sb([128, 2, 2, 256], BF16, "c256")
    uvt, b_uvt = C.sb([128, 2, 4, 256], BF16, "uvt")
    ycc, b_ycc = C.sb([128, 2, 512], F32, "ycc")
    o1, b_o1 = C.sb([128, D], F32, "o1")
    S.dma("sp", idn[:], ident, writes=[b_idn])
    S.dma("sp", bfB[:], bf[0, :].partition_broadcast(128), writes=[b_bf])
    S.dma("sp", c256[:], cs256, writes=[b_c256])
    S.dma("sp", uvt[:], uvc.rearrange("(tc p) g c -> p tc g c", p=128), writes=[b_uvt])
    woutv = wout.rearrange("(k p) n -> p k n", p=128)
    if not preloaded:
        for q in range(8):
            S.dma("pool", wo[:, 2 * q:2 * q + 2, :], woutv[:, 2 * q:2 * q + 2, :], writes=[b_wo])
    S.dma_dyn("sp", lambda E: (t.ycL, t.ycG[bass.ds(E.JG, 8192), :]), reads=[t.b_ycG], writes=[b_ycL])

    pz = [C.ps([128, 512], F32, f"pz{i}") for i in range(4)]
    _pb = [C.ps([128, 512], F32, f"ptrb{i}") for i in range(2)]
    ptr = [(_pb[i][0][:].bitcast(BF16)[:, 0:512].rearrange("p (j d) -> p j d", d=128), _pb[i][1]) for i in range(2)]

    def load_gg(i):
        ggB, b_gg = ggBs[i]
        S.dma("sp", o1[:], gpost[0, :].partition_broadcast(128), writes=[b_o1])
        S.dma("sp", ggB[:].rearrange("p (j c) -> p j c", c=512), t.moG5[:, i, l, 2, :].partition_broadcast(128),
              writes=[b_gg])
        S.op("dve", lambda E: E.tensor_tensor(out=ggB[:], in0=ggB[:], in1=o1[:], op=ALU.mult),
             reads=[b_gg, b_o1], writes=[b_gg])

    load_gg(0)
    if not last:
        load_gg(1)

    for kc in range(2):
        p, b_p = pz[kc]
        n = 0
        for tc in range(2):
            for ri in range(2):
                S.op("pe", lambda E, kc=kc, tc=tc, ri=ri, p=p, n=n: E.matmul(
                    out=p[:], lhsT=c256[:, tc, ri, kc * 128:(kc + 1) * 128], rhs=uvt[:, tc, :, ri * 128:(ri + 1) * 128],
                    start=(n == 0), stop=(n == 3)),
                    reads=[b_c256, b_uvt], writes=[b_p], signal=(n == 3))
                n += 1
        S.op("dve", lambda E, kc=kc, p=p: E.tensor_copy(out=ycc[:, kc, :], in_=p[:]), reads=[b_p], writes=[b_ycc])

    xt = [C.sb([128, D], F32, f"xt{i}") for i in range(2)]
    ysb = [C.sb([128, D], BF16, f"ysb{i}") for i in range(2)]
    yct = [C.sb([128, 512], F32, f"yct{i}") for i in range(2)]
    gct = [C.sb([128, 512], BF16, f"gct{i}") for i in range(2)]
    yT = [C.sb([128, 16, 128], BF16, f"yT{i}") for i in range(2)]
    xo = [C.sb([128, D], F32, f"xo{i}") for i in range(2)]
    t1, b_t1 = C.sb([128, 512], F32, "t1")
    jk, b_jk = C.sb([128, 512], BF16, "jk")
    ss = [C.sb([128, 8], F32, f"ss{i}") for i in range(2)]

    zs = [C.sb([128, D], F32, f"zs{i}") for i in range(2)]
    order = ([] if last else [16, 17]) + list(range(16))

    def stA(n):
        c = order[n]
        x_t, b_x = xt[n % 2]
        y_t, b_yt = ysb[n % 2]
        yc_t, b_yct = yct[n % 2]
        g_t, b_gt = gct[n % 2]
        rows = slice(c * 128, (c + 1) * 128)
        S.dma("sp", x_t[:], xin[rows, :], writes=[b_x])
        S.dma("sp", y_t[:, 0:1536], yab[rows, :], writes=[b_yt])
        S.dma("sp", g_t[:], P[rows, 4608:5120], writes=[b_gt])
        if c < 16:
            S.dma("sp", yc_t[:].rearrange("p (g d) -> p g d", d=128),
                  ycL3[:, c * 128:(c + 1) * 128, :].rearrange("g p d -> p g d"), reads=[b_ycL], writes=[b_yct])
            src, b_src = yc_t[:], b_yct
        else:
            src, b_src = ycc[:, c - 16, :], b_ycc
        S.op("dve", lambda E: E.tensor_tensor(out=t1[:], in0=src, in1=bfB[:], op=ALU.add),
             reads=[b_src, b_bf], writes=[b_t1])
        S.op("dve", lambda E: E.tensor_tensor(out=y_t[:, 1536:2048], in0=t1[:], in1=g_t[:], op=ALU.mult),
             reads=[b_t1, b_gt], writes=[b_yt])

    def stB(n):
        y_t, b_yt = ysb[n % 2]
        T_t, b_T = yT[n % 2]
        for g4 in range(4):
            p, b_p = ptr[g4 % 2]
            for jj in range(4):
                k = 4 * g4 + jj
                S.op("pe", lambda E, k=k, jj=jj, p=p: E.transpose(out=p[:, jj, :], in_=y_t[:, k * 128:(k + 1) * 128],
                                                                 identity=idn[:]),
                     reads=[b_yt, b_idn], writes=[b_p], signal=(jj == 3))
            if g4 % 2 == 0:
                S.op("dve", lambda E, g4=g4, p=p: E.tensor_copy(out=T_t[:, 4 * g4:4 * g4 + 4, :], in_=p[:]),
                     reads=[b_p], writes=[b_T])
            else:
                S.op("act", lambda E, g4=g4, p=p: E.activation(out=T_t[:, 4 * g4:4 * g4 + 4, :], in_=p[:], func=AF.Copy),
                     reads=[b_p], writes=[b_T])

    def stC(n):
        T_t, b_T = yT[n % 2]
        z_t, b_z = zs[n % 2]
        s_t, b_s = ss[n % 2]
        for nb in range(4):
            p, b_p = pz[nb]
            cs_ = slice(nb * 512, (nb + 1) * 512)
            for k in range(16):
                S.op("pe", lambda E, k=k, nb=nb, p=p: E.matmul(out=p[:], lhsT=T_t[:, k, :],
                                                               rhs=wo[:, k, nb * 512:(nb + 1) * 512],
                                                               start=(k == 0), stop=(k == 15)),
                     reads=[b_T, b_wo], writes=[b_p], signal=(k == 15))
            S.op("dve", lambda E, p=p, cs_=cs_: E.tensor_copy(out=z_t[:, cs_], in_=p[:]), reads=[b_p], writes=[b_z])
            S.op("act", lambda E, nb=nb, cs_=cs_: E.activation(out=jk[:], in_=z_t[:, cs_], func=AF.Square,
                                                               accum_out=s_t[:, nb:nb + 1]),
                 reads=[b_z], writes=[b_jk, b_s])

    def stD(n):
        c = order[n]
        x_t, b_x = xt[n % 2]
        z_t, b_z = zs[n % 2]
        s_t, b_s = ss[n % 2]
        xo_t, b_xot = xo[n % 2]
        rows = slice(c * 128, (c + 1) * 128)
        S.op("dve", lambda E: E.tensor_reduce(out=s_t[:, 4:5], in_=s_t[:, 0:4], axis=AX.X, op=ALU.add),
             reads=[b_s], writes=[b_s])
        S.op("act", lambda E: E.activation(out=s_t[:, 5:6], in_=s_t[:, 4:5], func=AF.Sqrt, bias=EPS, scale=1.0 / D),
             reads=[b_s], writes=[b_s])
        S.op("dve", lambda E: E.reciprocal(out=s_t[:, 6:7], in_=s_t[:, 5:6]), reads=[b_s], writes=[b_s])
        ggB, b_gg = ggBs[0 if c < 16 else 1]
        S.op("dve", lambda E: E.scalar_tensor_tensor(out=o1[:], in0=z_t[:], scalar=s_t[:, 6:7], in1=ggB[:],
                                                     op0=ALU.mult, op1=ALU.mult),
             reads=[b_z, b_s, b_gg], writes=[b_o1])
        S.op("pool", lambda E: E.tensor_tensor(out=xo_t[:, 0:1024], in0=o1[:, 0:1024], in1=x_t[:, 0:1024], op=ALU.add),
             reads=[b_o1, b_x], writes=[b_xot])
        S.op("dve", lambda E: E.tensor_tensor(out=xo_t[:, 1024:D], in0=o1[:, 1024:D], in1=x_t[:, 1024:D], op=ALU.add),
             reads=[b_o1, b_x], writes=[b_xot])
        S.dma("sp", xout[rows, :], xo_t[:], reads=[b_xot], writes=[b_xo])

    N = len(order)
    stA(0)
    stB(0)
    for n in range(N):
        if n + 1 < N:
            stA(n + 1)
        stC(n)
        if n + 1 < N:
            stB(n + 1)
        stD(n)
    return b_xo


class _T:
    pass


GROUPS = [[0, 1, 2, 3], [4, 5, 6, 7]]


def build_fused(stop=None, dbg=None, nl=4):
    nc = new_nc()
    C = Ctx(nc)
    S = C.S
    t = _T()
    t.xin = C.din("xin", [NTOK, D], F32)
    t.cT = C.din("cT", [128, 16, 2], F32)
    t.wm = C.din("wm", [4, D, 1536], F32)
    t.bm = C.din("bm", [2, 6144], F32)
    t.gpre = C.din("gpre", [4, D], F32)
    t.gpost = C.din("gpost", [4, D], F32)
    t.win = C.din("win", [4, D, INC], F32)
    t.wout = C.din("wout", [4, D, D], F32)
    t.gsgu = C.din("gsgu", [4, 512], F32)
    t.wf = C.din("wf", [4, 128, 4, 128], F32)
    t.wsT = C.din("wsT", [4, 128, 4, 128], F32)
    t.bsT = C.din("bsT", [4, 128, 4], F32)
    t.sink = C.din("sink", [4, 8], F32)
    t.bf = C.din("bf", [4, 512], F32)
    t.masks = C.din("masks", [3, 128, 384], F32)
    t.ident = C.din("ident", [128, 128], BF16)
    t.cs128 = C.din("cs128", [128, 256], BF16)
    t.cosT = C.din("cosT", [2048, 128], F32)
    t.sinS = C.din("sinS", [2048, 128], F32)
    t.f128 = C.din("f128", [128, 3, 128], BF16)
    t.G = C.din("G", [128, 128, 64], BF16)
    t.cs256 = C.din("cs256", [128, 2, 2, 256], BF16)
    t.xout = C.dout("xout", [2048, D], F32)
    xsA, _ = C.dscr("xsA", [NTOK, D], F32)
    xsB, _ = C.dscr("xsB", [NTOK, D], F32)
    t.xsA, t.xsB = xsA, xsB
    t.P, _ = C.dscr("P", [NTOK, INC], BF16)
    t.UVl, _ = C.dscr("UVl", [4 * 2048, 256], BF16)
    t.UVl3 = t.UVl.rearrange("(g n) c -> g n c", g=4)
    t.UVg, b_UVg = C.dscr("UVg", [16 * 2048, 256], BF16)
    t.b_UVg = b_UVg
    t.UVc, _ = C.dscr("UVc", [256, 4, 256], BF16)
    t.kvE, _ = C.dscr("kvE", [256, 512], BF16)
    t.kvG, b_kvG = C.dscr("kvG", [1024, 512], BF16)
    t.b_kvG = b_kvG
    t.ycB, _ = C.dscr("ycB", [8192, 128], F32)
    t.ycG, b_ycG = C.dscr("ycG", [4 * 8192, 128], F32)
    t.b_ycG = b_ycG
    t.ycL, _ = C.dscr("ycL", [4 * 2048, 128], F32)
    t.UVsel, _ = C.dscr("UVsel", [8192, 256], BF16)
    t.mo, _ = C.dscr("mo", [2, 6144], F32)
    t.moG, b_moG = C.dscr("moG", [8, 6144], F32)
    t.moG5 = t.moG.rearrange("(j r) (l v c) -> j r l v c", r=2, l=4, v=3)
    t.yab, _ = C.dscr("yab", [NTOK, 1536], BF16)
    t.bd, _ = C.dscr("bd", [128, 64, 2, 128], BF16)

    def finish(b_out):
        if dbg is not None:
            src = getattr(t, dbg)
            o = C.dout("dbg", list(src.shape), src.dtype)
            b_o = Buf("dbg")
            S.dma("sp", o, src, writes=[b_o])
            S.wait_all("sp", [b_o])
        elif b_out is not None:
            S.wait_all("sp", [b_out])
        S.emit()
        return nc

    ph_M(C, t)
    C.phase_end()
    if stop == "M":
        return finish(None)
    S.coll("AllGather", GROUPS, t.mo, t.moG, writes=[b_moG])
    C.phase_end()
    if stop == "MG":
        return finish(None)
    xs = [t.xin, xsA, xsA, xsA]
    b_out = None
    for l in range(nl):
        last = l == 3
        xin = xs[l]
        xout = t.xout if last else xs[l + 1]
        ph_A1(C, t, l, xin)
        C.phase_end()
        if stop == "A1" or stop == (l, "A1"):
            return finish(None)
        S.coll("AllGather", GROUPS, t.kvE, t.kvG, writes=[b_kvG])
        for g in range(4):
            S.coll("AllGather", GROUPS, t.UVl[g * 2048:(g + 1) * 2048, :], t.UVg[g * 8192:(g + 1) * 8192, :],
                   writes=[b_UVg])
        C.phase_end(skip_cc=True)
        if stop == "A1G" or stop == (l, "A1G"):
            return finish(None)
        allc = list(range(16)) + ([] if last else [16, 17])
        ph_A2(C, t, l, allc[1:8] + allc[0:1])
        C.phase_end()
        if stop == "A2" or stop == (l, "A2"):
            return finish(None)
        ph_B(C, t)
        C.phase_end()
        if stop == "B" or stop == (l, "B"):
            return finish(None)
        for jj in range(4):
            S.coll("AllGather", GROUPS, t.ycB[jj * 2048:(jj + 1) * 2048, :], t.ycG[jj * 8192:(jj + 1) * 8192, :],
                   writes=[t.b_ycG])
        C.phase_end(skip_cc=True)
        t.wo = C.sb([128, 16, D], BF16, "wo")
        woutv = t.wout[l].rearrange("(k p) n -> p k n", p=128)
        for q in range(8):
            S.dma("pool", t.wo[0][:, 2 * q:2 * q + 2, :], woutv[:, 2 * q:2 * q + 2, :], reads=[t.b_ycG],
                  writes=[t.wo[1]])
        ph_A2(C, t, l, allc[8:])
        C.phase_end(keep_first=1)
        b_out = ph_C(C, t, l, xin, xout, last)
        C.phase_end()
        if stop == "C" or stop == (l, "C"):
            return finish(None)
    return finish(b_out)


_NC = []


def kernel(x, c, ctx, c_ctx, w_mod, b_mod, g_pre, g_post, w_in, w_out, g_sgu, w_sgu, b_sgu, sink,
           w_fourier, b_fourier):
    f32 = lambda a: np.ascontiguousarray(np.asarray(a, dtype=np.float32))
    x, c, ctx, c_ctx = f32(x), f32(c), f32(ctx), f32(c_ctx)
    w_mod, b_mod, g_pre, g_post, w_in, w_out = f32(w_mod), f32(b_mod), f32(g_pre), f32(g_post), f32(w_in), f32(w_out)
    g_sgu, w_sgu, b_sgu, sink, w_fourier, b_fourier = f32(g_sgu), f32(w_sgu), f32(b_sgu), f32(sink), f32(w_fourier), f32(b_fourier)
    if not _NC:
        _NC.append(build_fused())
    nc = _NC[0]
    common = {
        "gpre": g_pre, "gpost": g_post, "win": w_in, "wout": w_out, "gsgu": g_sgu,
        "wf": np.ascontiguousarray(w_fourier.transpose(0, 2, 1, 3)),
        "wsT": np.ascontiguousarray(w_sgu.transpose(0, 3, 1, 2)),
        "bsT": np.ascontiguousarray(b_sgu.transpose(0, 2, 1)),
        "sink": sink, "bf": np.ascontiguousarray(b_fourier.reshape(4, 512)),
        "ident": const_ident(), "cs128": const_cs128(), "f128": const_f128(), "G": const_G(), "cs256": const_cs256(),
    }
    ropes = [rope_tables(j) for j in range(4)]
    wm4 = w_mod.reshape(4, D, 3, 4, 512)
    bm4 = b_mod.reshape(4, 3, 4, 512)
    ims = []
    for i in range(8):
        b, j = i // 4, i % 4
        cT = np.ascontiguousarray(np.stack([c[b], c_ctx]).T.reshape(16, 128, 2).transpose(1, 0, 2))
        d = dict(common)
        d.update({
            "xin": np.ascontiguousarray(np.concatenate([x[b, 2048 * j:2048 * (j + 1)], ctx[b]], axis=0)),
            "cT": cT,
            "wm": np.ascontiguousarray(wm4[:, :, :, j, :].reshape(4, D, 1536)),
            "bm": np.ascontiguousarray(np.broadcast_to(bm4[:, :, j, :].reshape(1, 6144), (2, 6144))),
            "masks": const_masks(j == 0, j == 3), "cosT": ropes[j][0], "sinS": ropes[j][1],
        })
        ims.append(d)
    res = run_bass_kernel_spmd(nc, ims, core_ids=list(range(8)))
    out = np.empty((2, 8192, D), dtype=np.float32)
    for i in range(8):
        b, j = i // 4, i % 4
        out[b, 2048 * j:2048 * (j + 1)] = res.results[i]["xout"]
    return out
```
